# Optimizing a Trainium2 kernel written in Bass

```python
import jax, jax.numpy as jnp
from jax import lax
import numpy as np

D_MODEL = 1024
BATCH = 4
SEQ = 8192
DEPTH = 4

GRID_W = 64
CTX_LEN = 256
N_BRANCH = 4
BRANCH_W = 512

RW_HEADS = 8
RW_HEAD = 64
RW_C = RW_HEADS * RW_HEAD
RW_DECAY_LORA = 64
RW_ICLR_LORA = 64
RW_GN_EPS = 64e-5
RW_SHIFT_W = 3 * RW_C + RW_DECAY_LORA + RW_ICLR_LORA
CONV_CH = 512
CONV_K = 31
CONV_LN_EPS = 1e-5
WA_HEADS = 8
WA_KV_HEADS = 2
WA_HEAD = 64
WA_GROUP = WA_HEADS // WA_KV_HEADS
WINDOW = 128
BLOCK = 128
WA_SCALE = WA_HEAD ** -0.5
MLA_HEADS = 8
MLA_Q_RANK = 256
MLA_KV_RANK = 128
MLA_NOPE = 64
MLA_ROPE = 32
MLA_V = 64
MLA_SCALE = (MLA_NOPE + MLA_ROPE) ** -0.5

ROPE_BASE = 10000.0
NORM_EPS = 1e-6

IN_SPLITS = (
    ("rw", RW_SHIFT_W),
    ("cv_glu", 2 * CONV_CH),
    ("wa_q", WA_HEADS * WA_HEAD),
    ("wa_k", WA_KV_HEADS * WA_HEAD),
    ("wa_v", WA_KV_HEADS * WA_HEAD),
    ("mla_dq", MLA_Q_RANK),
    ("mla_dkv", MLA_KV_RANK),
    ("mla_kr", MLA_ROPE),
    ("z", N_BRANCH * BRANCH_W),
    ("gate", N_BRANCH * D_MODEL),
)
N_IN = sum(size for _, size in IN_SPLITS)

kernel_name = "hybrid_rwkv7_conformer_swa_mla_diffusion_trunk"


def rmsnorm(t, w):
    tf = t.astype(jnp.float32)
    tf = tf * lax.rsqrt(jnp.mean(tf * tf, axis=-1, keepdims=True) + NORM_EPS)
    return tf.astype(t.dtype) * w


def heads(t, n):
    return t.reshape(t.shape[:-1] + (n, t.shape[-1] // n))


def split_proj(p):
    parts = {}
    off = 0
    for name, size in IN_SPLITS:
        parts[name] = p[..., off:off + size]
        off += size
    return parts


def axial_rope(t, row_pos, col_pos):
    d = t.shape[-1]
    q4 = d // 4
    freqs = ROPE_BASE ** (-jnp.arange(q4, dtype=jnp.float32) / q4)
    mid = (1,) * (t.ndim - 3)
    pieces = []
    for part, pos in ((t[..., : d // 2], row_pos), (t[..., d // 2:], col_pos)):
        ang = pos.astype(jnp.float32)[:, None] * freqs[None, :]
        cos = jnp.cos(ang).reshape((ang.shape[0],) + mid + (q4,))
        sin = jnp.sin(ang).reshape((ang.shape[0],) + mid + (q4,))
        p1 = part[..., :q4].astype(jnp.float32)
        p2 = part[..., q4:].astype(jnp.float32)
        pieces += [p1 * cos - p2 * sin, p1 * sin + p2 * cos]
    return jnp.concatenate(pieces, axis=-1).astype(t.dtype)


def centred_shift_mix(f, mu):
    fp = jnp.pad(f, ((0, 0), (1, 1), (0, 0)))
    nb = 0.5 * (fp[:, :-2] + fp[:, 2:])
    return f + mu * (nb - f)


def rwkv_prepare(rw, mu, w0, w_up, a0, a_up, k_k, k_a):
    rw = centred_shift_mix(rw, mu)
    r, k, v, wd, ad = jnp.split(rw, [RW_C, 2 * RW_C, 3 * RW_C, 3 * RW_C + RW_DECAY_LORA], axis=-1)
    w_pre = (w0[:, None, None, :] + jnp.einsum('btr,grc->gbtc', jnp.tanh(wd), w_up)).astype(jnp.float32)
    w_log = -jax.nn.softplus(-w_pre) - 0.5
    decay = heads(jnp.exp(-jnp.exp(w_log)), RW_HEADS)
    a = heads(jax.nn.sigmoid((a0[:, None, None, :] + jnp.einsum('btr,grc->gbtc', ad, a_up)).astype(jnp.float32)), RW_HEADS)
    kk = heads(k * k_k, RW_HEADS).astype(jnp.float32)
    kk = kk / jnp.maximum(jnp.sqrt(jnp.sum(kk * kk, axis=-1, keepdims=True)), 1e-12)
    k_dir = heads(k, RW_HEADS)[None] * (1.0 + (a - 1.0) * heads(k_a, RW_HEADS))
    return heads(r, RW_HEADS), heads(v, RW_HEADS), kk, decay, a, k_dir


def wkv_scan(s0, w, kk, b, k, v, r, reverse):
    seq = [w, kk, b, k, v] + ([] if r is None else [r])
    xs = tuple(jnp.swapaxes(t, 0, 1).astype(jnp.float32) for t in seq)

    def step(s, inp):
        w_t, kk_t, b_t, k_t, v_t = inp[:5]
        sa = jnp.einsum('bhvk,bhk->bhv', s, kk_t)
        s = s * w_t[:, :, None, :] - sa[..., None] * b_t[:, :, None, :] + v_t[..., None] * k_t[:, :, None, :]
        o = None if r is None else jnp.einsum('bhvk,bhk->bhv', s, inp[5])
        return s, o

    s, o = lax.scan(step, s0, xs, reverse=reverse)
    return s, (None if o is None else jnp.swapaxes(o, 0, 1))


def rwkv_mix(prep, s0, r_k, gn_w, gn_b, readout):
    r, v, kk, decay, a, k_dir = prep
    finals, outs = [], []
    for d, rev in ((0, False), (1, True)):
        s, o = wkv_scan(s0[d], decay[d], kk, kk * a[d], k_dir[d], v, r if readout else None, rev)
        finals.append(s)
        outs.append(o)
    s_fin = jnp.stack(finals)
    if not readout:
        return None, s_fin
    o = outs[0] + outs[1]
    mu = jnp.mean(o, axis=-1, keepdims=True)
    var = jnp.mean(jnp.square(o - mu), axis=-1, keepdims=True)
    bsz, t = o.shape[:2]
    on = ((o - mu) * lax.rsqrt(var + RW_GN_EPS)).reshape(bsz, t, RW_C) * gn_w + gn_b
    rf, vf = r.astype(jnp.float32), v.astype(jnp.float32)
    bonus = sum(jnp.sum(rf * k_dir[d] * r_k, axis=-1, keepdims=True) * vf for d in (0, 1))
    y = on + bonus.reshape(bsz, t, RW_C)
    return y.astype(r.dtype), s_fin


def conv_module(glu, conv_w, conv_b, ln_w, ln_b):
    u = glu[..., :CONV_CH] * jax.nn.sigmoid(glu[..., CONV_CH:])
    u = lax.conv_general_dilated(u, conv_w[:, None, :], window_strides=(1,),
                                 padding=((CONV_K // 2, CONV_K // 2),),
                                 dimension_numbers=('NWC', 'WIO', 'NWC'),
                                 feature_group_count=CONV_CH) + conv_b
    uf = u.astype(jnp.float32)
    mu = jnp.mean(uf, axis=-1, keepdims=True)
    var = jnp.mean(jnp.square(uf - mu), axis=-1, keepdims=True)
    un = ((uf - mu) * lax.rsqrt(var + CONV_LN_EPS)).astype(u.dtype) * ln_w + ln_b
    return jax.nn.silu(un)


def band_mask(nb):
    qpos = jnp.arange(nb)[:, None, None] * BLOCK + jnp.arange(BLOCK)[None, :, None]
    kpos = (jnp.arange(nb)[:, None, None] - 1) * BLOCK + jnp.arange(3 * BLOCK)[None, None, :]
    return (jnp.abs(kpos - qpos) <= WINDOW) & (kpos >= 0) & (kpos < nb * BLOCK)


def window_attn(q, k, v, k_c, v_c, sink):
    bsz, t = q.shape[:2]
    nb = t // BLOCK
    qb = q.reshape(bsz, nb, BLOCK, WA_KV_HEADS, WA_GROUP, WA_HEAD)

    def band(u):
        up = jnp.pad(u, ((0, 0), (BLOCK, BLOCK), (0, 0), (0, 0))).reshape(bsz, nb + 2, BLOCK, WA_KV_HEADS, WA_HEAD)
        return jnp.concatenate([up[:, :-2], up[:, 1:-1], up[:, 2:]], axis=2)

    kb, vb = band(k), band(v)
    s_lat = jnp.einsum('bnqhgd,bnkhd->bnhgqk', qb, kb).astype(jnp.float32) * WA_SCALE
    s_lat = jnp.where(band_mask(nb)[:, None, None], s_lat, -jnp.inf)
    s_ctx = jnp.einsum('bnqhgd,bchd->bnhgqc', qb, k_c).astype(jnp.float32) * WA_SCALE
    sk = sink.astype(jnp.float32).reshape(WA_KV_HEADS, WA_GROUP)[:, :, None, None]
    m = jnp.maximum(jnp.maximum(jnp.max(s_lat, -1, keepdims=True), jnp.max(s_ctx, -1, keepdims=True)), sk)
    e_lat = jnp.exp(s_lat - m)
    e_ctx = jnp.exp(s_ctx - m)
    inv = 1.0 / (jnp.sum(e_lat, -1, keepdims=True) + jnp.sum(e_ctx, -1, keepdims=True) + jnp.exp(sk - m))
    o = (jnp.einsum('bnhgqk,bnkhd->bnqhgd', (e_lat * inv).astype(v.dtype), vb)
         + jnp.einsum('bnhgqc,bchd->bnqhgd', (e_ctx * inv).astype(v.dtype), v_c))
    return o.reshape(bsz, t, WA_HEADS * WA_HEAD)


def gqa_sink_dense(q, k, v, sink):
    bsz, nq = q.shape[:2]
    qg = q.reshape(bsz, nq, WA_KV_HEADS, WA_GROUP, WA_HEAD)
    s = jnp.einsum('bqhgd,bkhd->bhgqk', qg, k).astype(jnp.float32) * WA_SCALE
    sk = sink.astype(jnp.float32).reshape(WA_KV_HEADS, WA_GROUP)[:, :, None, None]
    m = jnp.maximum(jnp.max(s, -1, keepdims=True), sk)
    e = jnp.exp(s - m)
    p = e / (jnp.sum(e, -1, keepdims=True) + jnp.exp(sk - m))
    o = jnp.einsum('bhgqk,bkhd->bqhgd', p.astype(v.dtype), v)
    return o.reshape(bsz, nq, WA_HEADS * WA_HEAD)


def mla_q(dq, q_norm, q_up):
    q = heads(rmsnorm(dq, q_norm) @ q_up, MLA_HEADS)
    return q[..., :MLA_NOPE], q[..., MLA_NOPE:]


def mla_kv(dkv, kv_norm, kv_up):
    kv = heads(rmsnorm(dkv, kv_norm) @ kv_up, MLA_HEADS)
    return kv[..., :MLA_NOPE], kv[..., MLA_NOPE:]


def mla_attend(qn, qr, kn, kr, v):
    s = (jnp.einsum('bqhd,bkhd->bhqk', qn, kn) + jnp.einsum('bqhr,bkr->bhqk', qr, kr)).astype(jnp.float32) * MLA_SCALE
    p = jax.nn.softmax(s, axis=-1).astype(v.dtype)
    return jnp.einsum('bhqk,bkhd->bqhd', p, v)


def mla_latent(qn, qr, kn, kr, v, kn_c, kr_c, v_c):
    kn_all = jnp.concatenate([kn, kn_c], axis=1)
    kr_all = jnp.concatenate([kr, kr_c], axis=1)
    v_all = jnp.concatenate([v, v_c], axis=1)
    bsz, t = qn.shape[:2]
    nb = t // BLOCK

    def blocks(u):
        return jnp.swapaxes(u.reshape((bsz, nb, BLOCK) + u.shape[2:]), 0, 1)

    o = lax.map(lambda qs: mla_attend(qs[0], qs[1], kn_all, kr_all, v_all), (blocks(qn), blocks(qr)))
    return jnp.swapaxes(o, 0, 1).reshape(bsz, t, MLA_HEADS * MLA_V)


def merge(ys, z, gate, w_branch, w_out):
    m = 0.0
    for i, y in enumerate(ys):
        zi = z[..., i * BRANCH_W:(i + 1) * BRANCH_W]
        gi = gate[..., i * D_MODEL:(i + 1) * D_MODEL]
        m = m + jax.nn.sigmoid(gi) * ((y * jax.nn.silu(zi)) @ w_branch[i])
    return m @ w_out


def setup_inputs(seed: int = 0) -> dict:
    key = jax.random.key(seed)
    ks = jax.random.split(key, 32)
    D = D_MODEL

    def nrm(k, shape, s):
        return jax.random.normal(k, shape, jnp.float32) * s

    return {
        "x": nrm(ks[0], (BATCH, SEQ, D), 1.0),
        "c": nrm(ks[1], (BATCH, D), 1.0),
        "ctx": nrm(ks[2], (BATCH, CTX_LEN, D), 1.0),
        "c_ctx": nrm(ks[3], (D,), 1.0),
        "norm_w": 1.0 + nrm(ks[4], (DEPTH, D), 0.02),
        "ada_w": nrm(ks[5], (DEPTH, D, 3 * D), 0.5 * D ** -0.5),
        "ada_b": nrm(ks[6], (DEPTH, 3 * D), 0.02),
        "w_in": nrm(ks[7], (DEPTH, D, N_IN), D ** -0.5),
        "rwkv_mu": jax.random.uniform(ks[8], (DEPTH, RW_SHIFT_W), jnp.float32),
        "rwkv_w0": jax.random.uniform(ks[9], (DEPTH, 2, RW_C), jnp.float32, minval=-6.5, maxval=-1.0),
        "rwkv_w_up": nrm(ks[10], (DEPTH, 2, RW_DECAY_LORA, RW_C), 0.1 * RW_DECAY_LORA ** -0.5),
        "rwkv_a0": nrm(ks[11], (DEPTH, 2, RW_C), 0.1),
        "rwkv_a_up": nrm(ks[12], (DEPTH, 2, RW_ICLR_LORA, RW_C), 0.5 * RW_ICLR_LORA ** -0.5),
        "rwkv_k_k": 0.85 + nrm(ks[13], (DEPTH, RW_C), 0.05),
        "rwkv_k_a": 1.0 + nrm(ks[14], (DEPTH, RW_C), 0.05),
        "rwkv_r_k": nrm(ks[15], (DEPTH, RW_HEADS, RW_HEAD), 0.1),
        "rwkv_gn_w": 1.0 + nrm(ks[16], (DEPTH, RW_C), 0.02),
        "rwkv_gn_b": nrm(ks[17], (DEPTH, RW_C), 0.02),
        "conv_w": nrm(ks[18], (DEPTH, CONV_K, CONV_CH), CONV_K ** -0.5),
        "conv_b": nrm(ks[19], (DEPTH, CONV_CH), 0.02),
        "conv_ln_w": 1.0 + nrm(ks[20], (DEPTH, CONV_CH), 0.02),
        "conv_ln_b": nrm(ks[21], (DEPTH, CONV_CH), 0.02),
        "attn_sink": nrm(ks[22], (DEPTH, WA_HEADS), 0.5),
        "mla_q_norm": 1.0 + nrm(ks[23], (DEPTH, MLA_Q_RANK), 0.02),
        "mla_q_up": nrm(ks[24], (DEPTH, MLA_Q_RANK, MLA_HEADS * (MLA_NOPE + MLA_ROPE)), MLA_Q_RANK ** -0.5),
        "mla_kv_norm": 1.0 + nrm(ks[25], (DEPTH, MLA_KV_RANK), 0.02),
        "mla_kv_up": nrm(ks[26], (DEPTH, MLA_KV_RANK, MLA_HEADS * (MLA_NOPE + MLA_V)), MLA_KV_RANK ** -0.5),
        "w_branch": nrm(ks[27], (DEPTH, N_BRANCH, BRANCH_W, D), BRANCH_W ** -0.5),
        "w_out": nrm(ks[28], (DEPTH, D, D), D ** -0.5),
        "final_norm_w": 1.0 + nrm(ks[29], (D,), 0.02),
    }


def reference(x, c, ctx, c_ctx, norm_w, ada_w, ada_b, w_in, rwkv_mu, rwkv_w0, rwkv_w_up, rwkv_a0,
              rwkv_a_up, rwkv_k_k, rwkv_k_a, rwkv_r_k, rwkv_gn_w, rwkv_gn_b, conv_w, conv_b, conv_ln_w,
              conv_ln_b, attn_sink, mla_q_norm, mla_q_up, mla_kv_norm, mla_kv_up, w_branch, w_out,
              final_norm_w):
    bsz, t, _ = x.shape
    n_ctx = ctx.shape[1]
    rows = t // GRID_W
    row_pos = jnp.repeat(jnp.arange(rows), GRID_W)
    col_pos = jnp.arange(rows * GRID_W) % GRID_W
    s_zero = jnp.zeros((2, bsz, RW_HEADS, RW_HEAD, RW_HEAD), jnp.float32)
    silu_c = jax.nn.silu(c)
    silu_cc = jax.nn.silu(c_ctx)

    for l in range(DEPTH):
        last = l == DEPTH - 1
        mod_x = silu_c @ ada_w[l] + ada_b[l]
        mod_c = silu_cc @ ada_w[l] + ada_b[l]
        sh_x, sc_x, g_x = jnp.split(mod_x[:, None, :], 3, axis=-1)
        sh_c, sc_c, g_c = jnp.split(mod_c, 3)
        hx = rmsnorm(x, norm_w[l]) * (1.0 + sc_x) + sh_x
        hc = rmsnorm(ctx, norm_w[l]) * (1.0 + sc_c) + sh_c
        px = split_proj(hx @ w_in[l])
        pc = split_proj(hc @ w_in[l])

        rw_args = (rwkv_mu[l], rwkv_w0[l], rwkv_w_up[l], rwkv_a0[l], rwkv_a_up[l], rwkv_k_k[l], rwkv_k_a[l])
        out_args = (rwkv_r_k[l], rwkv_gn_w[l], rwkv_gn_b[l])
        ya_c, s_ctx = rwkv_mix(rwkv_prepare(pc["rw"], *rw_args), s_zero, *out_args, readout=not last)
        ya_x, _ = rwkv_mix(rwkv_prepare(px["rw"], *rw_args), s_ctx, *out_args, readout=True)

        conv_args = (conv_w[l], conv_b[l], conv_ln_w[l], conv_ln_b[l])
        yb_x = conv_module(px["cv_glu"], *conv_args)

        qw_x = axial_rope(heads(px["wa_q"], WA_HEADS), row_pos, col_pos)
        kw_x = axial_rope(heads(px["wa_k"], WA_KV_HEADS), row_pos, col_pos)
        vw_x = heads(px["wa_v"], WA_KV_HEADS)
        kw_c = heads(pc["wa_k"], WA_KV_HEADS)
        vw_c = heads(pc["wa_v"], WA_KV_HEADS)
        yc_x = window_attn(qw_x, kw_x, vw_x, kw_c, vw_c, attn_sink[l])

        qn_x, qr_x = mla_q(px["mla_dq"], mla_q_norm[l], mla_q_up[l])
        qr_x = axial_rope(qr_x, row_pos, col_pos)
        kn_x, vd_x = mla_kv(px["mla_dkv"], mla_kv_norm[l], mla_kv_up[l])
        kr_x = axial_rope(px["mla_kr"], row_pos, col_pos)
        kn_c, vd_c = mla_kv(pc["mla_dkv"], mla_kv_norm[l], mla_kv_up[l])
        kr_c = pc["mla_kr"]
        yd_x = mla_latent(qn_x, qr_x, kn_x, kr_x, vd_x, kn_c, kr_c, vd_c)

        x = x + g_x * merge((ya_x, yb_x, yc_x, yd_x), px["z"], px["gate"], w_branch[l], w_out[l])

        if not last:
            yb_c = conv_module(pc["cv_glu"], *conv_args)
            yc_c = gqa_sink_dense(heads(pc["wa_q"], WA_HEADS), kw_c, vw_c, attn_sink[l])
            qn_c, qr_c = mla_q(pc["mla_dq"], mla_q_norm[l], mla_q_up[l])
            yd_c = mla_attend(qn_c, qr_c, kn_c, kr_c, vd_c).reshape(bsz, n_ctx, MLA_HEADS * MLA_V)
            ctx = ctx + g_c * merge((ya_c, yb_c, yc_c, yd_c), pc["z"], pc["gate"], w_branch[l], w_out[l])

    return rmsnorm(x, final_norm_w)
```

```python
import contextlib
import numpy as np
import concourse.bass as bass
import concourse.mybir as mybir
from concourse.bass_utils import run_bass_kernel_spmd

F32 = mybir.dt.float32
BF16 = mybir.dt.bfloat16
AF = mybir.ActivationFunctionType
ALU = mybir.AluOpType
AX = mybir.AxisListType

import os
RWSTOP = int(os.environ.get("RWSTOP", "4"))
RWX = int(os.environ.get("RWX", "9"))
RWY = int(os.environ.get("RWY", "9"))
RWZ = int(os.environ.get("RWZ", "9"))
D = 1024
CTX = 256
NIN = 10016
EPS = 1e-6


class Buf:
    __slots__ = ("w", "rs", "rd")

    def __init__(self):
        self.w = None
        self.rs = {}
        self.rd = []


class TileT:
    __slots__ = ("t", "b")

    def __init__(self, t, b):
        self.t = t
        self.b = b

    def __getitem__(self, k):
        return self.t[k]


class Rot:
    def __init__(self, tiles):
        self.tiles = tiles
        self.i = 0

    def next(self):
        t = self.tiles[self.i % len(self.tiles)]
        self.i += 1
        return t


class KB:
    def __init__(self, nc, stack):
        self.nc = nc
        self.ops = []
        self.start = 0
        self.esem = {}
        self.ecnt = {}
        for e in ("pe", "dve", "act", "pool"):
            self.esem[e] = stack.enter_context(nc.semaphore("cs_" + e))
            self.ecnt[e] = 0
        self.dsems = {}
        self.dcnt = {}
        self.stack = stack
        self.waited = {e: {} for e in ("pe", "dve", "act", "pool", "sp")}
        self.semval = {}
        self.dbufs = {}

    def dsem(self, key):
        if key not in self.dsems:
            if not hasattr(self, "spare"):
                self.spare = []
                self.allsems = []
            if self.spare:
                sem, cnt = self.spare.pop()
            else:
                sem = self.stack.enter_context(self.nc.semaphore("ds_%d" % len(self.allsems)))
                cnt = 0
                self.allsems.append(sem)
            self.dsems[key] = sem
            self.dcnt[key] = cnt
        return key

    def dbuf(self, *key):
        if key not in self.dbufs:
            self.dbufs[key] = Buf()
        return self.dbufs[key]

    def op(self, e, fn, r=(), w=(), dkey=None):
        i = len(self.ops)
        deps = set()
        for b in r:
            b = b.b if isinstance(b, TileT) else b
            if b.w is not None:
                deps.add(b.w)
        for b in w:
            b = b.b if isinstance(b, TileT) else b
            if b.w is not None:
                deps.add(b.w)
            deps.update(b.rs.values())
            deps.update(b.rd)
        for b in r:
            b = b.b if isinstance(b, TileT) else b
            if dkey is not None:
                b.rd.append(i)
            else:
                b.rs[e] = i
        for b in w:
            b = b.b if isinstance(b, TileT) else b
            b.w = i
            b.rs = {}
            b.rd = []
        deps.discard(i)
        if dkey is not None:
            self.dsem(dkey)
        self.ops.append((e, fn, deps, dkey))

    def dma(self, e, out, in_, r, w, key):
        self.op(e, lambda q: q.dma_start(out=out, in_=in_), r, w, dkey=key)

    def flush(self):
        nc = self.nc
        ops = self.ops
        start = self.start
        n = len(ops)
        needs = [False] * (n - start)
        for i in range(start, n):
            e, fn, deps, dkey = ops[i]
            for d in deps:
                if d >= start and ops[d][3] is None:
                    if not (ops[d][0] == "pe" and e == "pe"):
                        needs[d - start] = True
        per = {e: [] for e in ("pe", "dve", "act", "pool", "sp")}
        for i in range(start, n):
            e, fn, deps, dkey = ops[i]
            per[e].append(i)
            if dkey is not None:
                self.dcnt[dkey] += 16
                self.semval[i] = (self.dsems[dkey], self.dcnt[dkey])
            elif needs[i - start]:
                self.ecnt[e] += 1
                self.semval[i] = (self.esem[e], self.ecnt[e])
        self.start = n
        if os.environ.get("KBSIM"):
            ptr = {e: 0 for e in per}
            done = set()
            total = sum(len(v) for v in per.values())
            ndone = 0
            while ndone < total:
                prog = False
                for e in per:
                    while ptr[e] < len(per[e]):
                        i = per[e][ptr[e]]
                        ok = True
                        for d in ops[i][2]:
                            if d < start:
                                continue
                            if ops[d][3] is None and ops[d][0] == "pe" and e == "pe":
                                continue
                            if d not in done:
                                ok = False
                                break
                            if ops[d][3] is None and d not in self.semval:
                                print("KBSIM: dep without semval", d, i)
                        if not ok:
                            break
                        done.add(i)
                        ptr[e] += 1
                        ndone += 1
                        prog = True
                if not prog:
                    print("KBSIM DEADLOCK", {e: (ptr[e], len(per[e])) for e in per})
                    break
            else:
                print("KBSIM ok", total)

        def emit(ename, eng):
            wd = self.waited[ename]
            for i in per[ename]:
                e, fn, deps, dkey = ops[i]
                waits = {}
                for d in deps:
                    de, _, _, dk = ops[d]
                    if dk is None:
                        if d < start:
                            continue
                        if de == "pe" and e == "pe":
                            continue
                    if dk is not None and d < start:
                        continue
                    sem, val = self.semval[d]
                    k = id(sem)
                    if k not in waits or waits[k][1] < val:
                        waits[k] = (sem, val)
                for k, (sem, val) in waits.items():
                    if wd.get(k, 0) < val:
                        eng.wait_ge(sem, val)
                        wd[k] = val
                ins = fn(eng)
                if dkey is not None:
                    ins.then_inc(self.dsems[dkey], 16)
                elif i in self.semval:
                    ins.then_inc(self.esem[e], 1)

        with nc.Block() as block:
            if per["pe"]:
                @block.tensor
                def _(t):
                    emit("pe", t)
            if per["dve"]:
                @block.vector
                def _(t):
                    emit("dve", t)
            if per["act"]:
                @block.scalar
                def _(t):
                    emit("act", t)
            if per["pool"]:
                @block.gpsimd
                def _(t):
                    emit("pool", t)
            @block.sync
            def _(t):
                emit("sp", t)
                for key, sem in self.dsems.items():
                    if self.dcnt[key] > 0:
                        t.wait_ge(sem, self.dcnt[key])
        for key, sem in self.dsems.items():
            self.spare.append((sem, self.dcnt[key]))
        self.dsems = {}
        self.dcnt = {}

    def finish(self):
        nc = self.nc
        pass

    def mm(self, out, pairs, r, w, first=True, last=True):
        pairs = list(pairs)

        def fn(e):
            n = len(pairs)
            ins = None
            for i, (a, b) in enumerate(pairs):
                ins = e.matmul(out, a, b, start=(first and i == 0), stop=(last and i == n - 1))
            return ins
        self.op("pe", fn, r, w)

    def act(self, out, in_, func, r, w, bias=None, scale=None):
        kw = {}
        if bias is not None:
            kw["bias"] = bias
        if scale is not None:
            kw["scale"] = scale
        self.op("act", lambda e: e.activation(out, in_, func, **kw), r, w)

    def tt(self, eng, out, a, b, op, r, w):
        self.op(eng, lambda e: e.tensor_tensor(out, a, b, op), r, w)

    def ts(self, eng, out, a, s1, s2, op0, op1, r, w):
        if op1 is None:
            self.op(eng, lambda e: e.tensor_scalar(out, a, s1, s2, op0), r, w)
        else:
            self.op(eng, lambda e: e.tensor_scalar(out, a, s1, s2, op0, op1), r, w)

    def stt(self, eng, out, a, s, b, op0, op1, r, w):
        self.op(eng, lambda e: e.scalar_tensor_tensor(out, a, s, b, op0, op1), r, w)

    def copy(self, eng, out, in_, r, w):
        if eng == "act":
            self.op(eng, lambda e: e.copy(out, in_), r, w)
        else:
            self.op(eng, lambda e: e.tensor_copy(out, in_), r, w)

    def memset(self, eng, out, val, w):
        self.op(eng, lambda e: e.memset(out, val), (), w)


O_RW = 0
O_GLU = 1664
O_WQ = 2688
O_WK = 3200
O_WV = 3328
O_DQ = 3456
O_DKV = 3712
O_KR = 3840
O_Z = 3872
O_GATE = 5920
NA = 3872

VEC = {}
_o = 0
for _name, _n in (("norm_w", 8), ("ada_b", 24), ("mu", 13), ("w0", 8), ("a0", 8), ("k_k", 4), ("k_a", 4),
                  ("r_k", 4), ("conv_b", 4), ("ln_w", 4), ("ln_b", 4), ("q_norm", 2), ("kv_norm", 1),
                  ("conv_w", 124)):
    VEC[_name] = (_o, _n)
    _o += _n
NVEC = _o


class Cfg:
    def __init__(self, T, depth, branches=(1, 1, 1, 1), debug=False):
        self.T = T
        self.TT = T + CTX
        self.depth = depth
        self.branches = branches
        self.debug = debug
        self.tiles = [(i * 512, 512) for i in range(T // 512)] + [(T, CTX)]


def build(cfg):
    nc = bass.Bass("TRN2", target_bir_lowering=False)
    T, TT, L = cfg.T, cfg.TT, cfg.depth
    skind = "ExternalOutput" if cfg.debug else "Internal"

    def din(name, shape, dt=F32):
        return nc.dram_tensor(name, list(shape), dt, kind="ExternalInput").ap()

    def dscr(name, shape, dt):
        return nc.dram_tensor(name, list(shape), dt, kind=skind).ap()

    xin = din("xin", [8, 128, TT])
    cvec = din("cvec", [128, 8, 2])
    vecs = din("vecs", [L, 128, NVEC])
    ada_w = din("ada_w", [L, D, 3 * D])
    w_in = din("w_in", [L, D, NIN])
    w_branch = din("w_branch", [L, 4, 512, D])
    w_out = din("w_out", [L, D, D])
    q_up = din("q_up", [L, 256, 768])
    kv_up = din("kv_up", [L, 128, 1024])
    fnw = din("fnw", [128, 8])
    cosW = din("cosW", [128, TT])
    sinW = din("sinW", [128, TT])
    cosM = din("cosM", [96, TT])
    sinM = din("sinM", [96, TT])
    ident_in = din("ident", [128, 128])
    wmask_in = din("wmask", [128, 2, 512])
    sink_in = din("sinkrow", [L, 2, 1, 512])
    yout = nc.dram_tensor("yout", [8, 128, T], F32, kind="ExternalOutput").ap()
    w_up = din("w_up", [L, 2, 64, 512])
    a_up = din("a_up", [L, 2, 64, 512])
    bdones_in = din("bdones", [128, 128])
    rmask_in = din("rmask", [128, 4, 512])
    resetm_in = din("resetm", [128, 512])
    gnbc = din("gnbc", [L, 2, 128, 512])
    NC_ = TT // 64
    NPB_ = TT // 128
    McD = dscr("McD", [2, 4, NC_, 128, 128], F32)
    NcD = dscr("NcD", [2, 4, NC_, 128, 64], F32)
    GD = dscr("GD", [2, 4, NC_, 128, 64], BF16)
    QhD = dscr("QhD", [2, 4, NPB_, 128, 256], BF16)
    OLD = dscr("OLD", [NPB_, 128, 512], F32)
    bvD = dscr("bvD", [4, 128, TT], F32)

    xs = dscr("xs", [8, 128, TT], F32)
    hT = dscr("hT", [8, 128, TT], BF16)
    rwT = dscr("rwT", [13, 128, TT], F32)
    uT = dscr("uT", [4, 128, TT], BF16)
    qwT = dscr("qwT", [8, 64, TT], BF16)
    kwT = dscr("kwT", [2, 64, TT], BF16)
    vw = dscr("vw", [TT, 128], BF16)
    qmT = dscr("qmT", [8, 96, TT], BF16)
    kmT = dscr("kmT", [8, 96, TT], BF16)
    vm = dscr("vm", [TT, 512], BF16)
    yT = dscr("yT", [4, 4, 128, TT], BF16)

    with contextlib.ExitStack() as top:
        kb = KB(nc, top)

        uid = [0]

        def sb(name, shape, dt, st=top):
            uid[0] += 1
            t = st.enter_context(nc.sbuf_tensor("s%d_%s" % (uid[0], name), list(shape), dt))
            return TileT(t, Buf())

        def sbrot(name, shape, dt, n, st):
            return Rot([sb("%s%d" % (name, i), shape, dt, st) for i in range(n)])

        psb = [TileT(top.enter_context(nc.psum_tensor("ps%d" % i, [128, 512], F32)), Buf()) for i in range(8)]
        psrot = Rot(psb)

        ident = sb("ident", [128, 128], F32)
        identb = sb("identb", [128, 128], BF16)
        onesb = sb("onesb", [128, 128], BF16)
        vec = sb("vec", [128, L, NVEC], F32)
        modA = sb("modA", [128, L, 8, 2], F32)
        modB = sb("modB", [128, L, 8, 2], F32)
        modG = sb("modG", [128, L, 8, 2], F32)
        fnw_t = sb("fnw", [128, 8], F32)
        onesf = sb("onesf", [128, 128], F32)
        wmask = sb("wmask", [128, 2, 512], BF16)
        bdones = sb("bdones", [128, 128], BF16)
        rmask = sb("rmask", [128, 4, 512], BF16)
        resetm = sb("resetm", [128, 512], F32)

        with contextlib.ExitStack() as st:
            kb.dma("sp", ident[:], ident_in[:, :], (), (ident,), "c0")
            kb.dma("sp", fnw_t[:], fnw[:, :], (), (fnw_t,), "c0")
            for l in range(L):
                kb.dma("sp", vec[:, l, :], vecs[l, :, :], (), (vec,), "c0")
            kb.copy("dve", identb[:], ident[:], (ident,), (identb,))
            kb.memset("dve", onesb[:], 1.0, (onesb,))
            kb.memset("dve", onesf[:], 1.0, (onesf,))
            kb.dma("pool", wmask[:], wmask_in[:, :, :], (), (wmask,), "c1")
            kb.dma("pool", bdones[:], bdones_in[:, :], (), (bdones,), "c1")
            kb.dma("pool", rmask[:], rmask_in[:, :, :], (), (rmask,), "c1")
            kb.dma("sp", resetm[:], resetm_in[:, :], (), (resetm,), "c0")
            cv = sb("cv", [128, 8, 2], F32, st)
            sc2 = sb("sc2", [128, 8, 2], F32, st)
            kb.dma("sp", cv[:], cvec[:, :, :], (), (cv,), "c0")
            kb.act(sc2[:], cv[:], AF.Silu, (cv,), (sc2,))
            awr = sbrot("aw", [128, 8, 512], F32, 2, st)
            mod = sb("mod", [128, 24, 2], F32, st)
            for l in range(L):
                ps = psrot.next()
                for jj in range(6):
                    aw = awr.next()
                    for k in range(8):
                        kb.dma("sp", aw[:, k, :], ada_w[l, k * 128:(k + 1) * 128, jj * 512:(jj + 1) * 512], (), (aw,),
                               "aw%d" % (jj % 2))
                    for j4 in range(4):
                        j = jj * 4 + j4
                        kb.mm(ps[:, 2 * j:2 * j + 2],
                              [(aw[:, k, j4 * 128:(j4 + 1) * 128], sc2[:, k, :]) for k in range(8)], (aw, sc2), (ps,))
                ob, _ = VEC["ada_b"]
                on, _ = VEC["norm_w"]
                for c in range(2):
                    kb.tt("dve", mod[:, :, c], ps[:, c:48:2], vec[:, l, ob:ob + 24], ALU.add, (ps, vec), (mod,))
                kb.copy("dve", modB[:, l, :, :], mod[:, 0:8, :], (mod,), (modB,))
                kb.copy("dve", modG[:, l, :, :], mod[:, 16:24, :], (mod,), (modG,))
                for c in range(2):
                    kb.stt("dve", modA[:, l, :, c], mod[:, 8:16, c], 1.0, vec[:, l, on:on + 8], ALU.add, ALU.mult,
                           (mod, vec), (modA,))
            for k in range(8):
                kb.dma("sp", xs[k, :, :], xin[k, :, :], (), (kb.dbuf("xs", k),), "xc")
            if cfg.debug:
                dbgm = nc.dram_tensor("dbgm", [128, 3, L * 16], F32, kind="ExternalOutput").ap()
                for i3, mt in enumerate((modA, modB, modG)):
                    kb.dma("sp", dbgm[:, i3, :], mt[:].rearrange("p l k c -> p (l k c)"), (mt,), (kb.dbuf("dbgm"),), "xc")
            kb.flush()

        for l in range(L):
            last = (l == L - 1)
            tiles = cfg.tiles[:-1] if False else cfg.tiles
            phase_A(nc, kb, cfg, l, locals())
            kb.flush()
            if cfg.branches[0]:
                phase_RW(nc, kb, cfg, l, locals())
            if cfg.branches[1]:
                phase_conv(nc, kb, cfg, l, locals())
                kb.flush()
            if cfg.branches[2]:
                phase_WA(nc, kb, cfg, l, locals())
                kb.flush()
            if cfg.branches[3]:
                phase_MLA(nc, kb, cfg, l, locals())
                kb.flush()
            phase_M(nc, kb, cfg, l, locals())
            kb.flush()

        with contextlib.ExitStack() as st:
            xr = sbrot("fx", [128, 8, 512], F32, 2, st)
            sqr = sbrot("fsq", [128, 8, 512], BF16, 2, st)
            rsr = sbrot("frs", [128, 512], F32, 2, st)
            orr = sbrot("fo", [128, 8, 512], F32, 2, st)
            for ti, (c0, n) in enumerate(cfg.tiles[:-1]):
                x_t = xr.next(); sq = sqr.next(); rs = rsr.next(); o_t = orr.next()
                kb.dma("sp", x_t[:, :, :n], xs[:, :, c0:c0 + n].rearrange("k p n -> p k n"),
                       [kb.dbuf("xs", k) for k in range(8)], (x_t,), "fx%d" % (ti % 2))
                kb.act(sq[:, :, :n], x_t[:, :, :n], AF.Square, (x_t,), (sq,))
                ps = psrot.next()
                kb.mm(ps[:, :n], [(onesb[:], sq[:, k, :n]) for k in range(8)], (onesb, sq), (ps,))
                rstd(kb, rs[:, :n], ps[:, :n], 1.0 / D, EPS, (ps,), (rs,))
                for k in range(8):
                    kb.stt("dve", o_t[:, k, :n], x_t[:, k, :n], fnw_t[:, k:k + 1], rs[:, :n], ALU.mult, ALU.mult,
                           (x_t, rs, fnw_t), (o_t,))
                kb.dma("pool", yout[:, :, c0:c0 + n].rearrange("k p n -> p k n"), o_t[:, :, :n], (o_t,),
                       (kb.dbuf("yout", ti),), "fo%d" % (ti % 2))
            kb.flush()
        kb.finish()
    return nc


def rstd(kb, out, ssq, inv_n, eps, r, w):
    kb.ts("dve", out, ssq, inv_n, eps, ALU.mult, ALU.add, r, w)
    kb.act(out, out, AF.Sqrt, w, w)
    kb.op("dve", lambda e: e.reciprocal(out, out), w, w)


def phase_A(nc, kb, cfg, l, env):
    T, TT = cfg.T, cfg.TT
    g = env
    psrot, vec, modA, modB = g["psrot"], g["vec"], g["modA"], g["modB"]
    onesb, identb = g["onesb"], g["identb"]
    sb, sbrot = g["sb"], g["sbrot"]
    xs, hT, rwT, uT = g["xs"], g["hT"], g["rwT"], g["uT"]
    qwT, kwT, vw, qmT, kmT, vm = g["qwT"], g["kwT"], g["vw"], g["qmT"], g["kmT"], g["vm"]
    w_in, q_up, kv_up = g["w_in"], g["q_up"], g["kv_up"]
    with contextlib.ExitStack() as st:
        W = sb("aW", [128, 8, NA], BF16, st)
        Wr = sb("aWr", [128, 8, 672], BF16, st)
        qup = sb("aqup", [128, 2, 768], BF16, st)
        qupr = sb("aqupr", [128, 2, 768], BF16, st)
        kvk = sb("akvk", [128, 8, 64], BF16, st)
        kvv = sb("akvv", [128, 8, 64], BF16, st)
        for k in range(8):
            kb.dma("pool", W[:, k, :], w_in[l, k * 128:(k + 1) * 128, 0:NA], (), (W,), "aW")
        for k in range(2):
            kb.dma("pool", qup[:, k, :], q_up[l, k * 128:(k + 1) * 128, :], (), (qup,), "aW")
        kvr = kv_up[l, :, :].rearrange("p (h two d) -> p h two d", h=8, two=2, d=64)
        kb.dma("pool", kvk[:], kvr[:, :, 0, :], (), (kvk,), "aW")
        kb.dma("pool", kvv[:], kvr[:, :, 1, :], (), (kvv,), "aW")

        def rot_cols(dst, src, nh, q4):
            d4 = dst.rearrange("p (h q s) -> p h q s", h=nh, q=4, s=q4)
            s4 = src.rearrange("p (h q s) -> p h q s", h=nh, q=4, s=q4)
            return d4, s4
        for k in range(8):
            for (do, so, nh, q4) in ((0, O_WQ, 8, 16), (512, O_WK, 2, 16), (640, O_KR, 1, 8)):
                wdt = nh * 4 * q4
                d4, s4 = rot_cols(Wr[:, k, do:do + wdt], W[:, k, so:so + wdt], nh, q4)
                for half in (0, 2):
                    kb.ts("pool", d4[:, :, half, :], s4[:, :, half + 1, :], -1.0, None, ALU.mult, None, (W,), (Wr,))
                    kb.copy("pool", d4[:, :, half + 1, :], s4[:, :, half, :], (W,), (Wr,))
        kb.memset("pool", qupr[:], 0.0, (qupr,))
        for k in range(2):
            d3 = qupr[:, k, :].rearrange("p (h c) -> p h c", h=8, c=96)
            s3 = qup[:, k, :].rearrange("p (h c) -> p h c", h=8, c=96)
            for half in (0, 2):
                a0 = 64 + half * 8
                kb.ts("pool", d3[:, :, a0:a0 + 8], s3[:, :, a0 + 8:a0 + 16], -1.0, None, ALU.mult, None, (qup,), (qupr,))
                kb.copy("pool", d3[:, :, a0 + 8:a0 + 16], s3[:, :, a0:a0 + 8], (qup,), (qupr,))

        xr = sbrot("ax", [128, 8, 512], F32, 1, st)
        sqr = sbrot("asq", [128, 8, 512], BF16, 2, st)
        rsr = sbrot("ars", [128, 512], F32, 2, st)
        tmr = sbrot("atm", [128, 512], F32, 3, st)
        hr = sbrot("ah", [128, 8, 512], BF16, 2, st)
        rwr = sbrot("arw", [128, 512], F32, 3, st)
        sgr = sbrot("asg", [128, 512], F32, 2, st)
        ur = sbrot("au", [128, 512], BF16, 3, st)
        csr = sbrot("acs", [128, 2, 512], F32, 1, st)
        cmr = sbrot("acm", [128, 2, 512], F32, 1, st)
        qor = sbrot("aqo", [128, 512], BF16, 4, st)
        vwr = sbrot("avw", [128, 4, 128], BF16, 2, st)
        dqr = sbrot("adq", [128, 2, 512], F32, 1, st)
        cqr = sbrot("acq", [128, 2, 512], BF16, 2, st)
        dkr = sbrot("adk", [128, 512], F32, 1, st)
        ckr = sbrot("ack", [128, 512], BF16, 2, st)
        kmr = sbrot("akm", [128, 512], BF16, 4, st)
        vmr = sbrot("avm", [128, 4, 512], BF16, 1, st)
        on, _ = VEC["norm_w"]
        oq, _ = VEC["q_norm"]
        okv, _ = VEC["kv_norm"]

        for ti, (c0, n) in enumerate(cfg.tiles):
            isc = 1 if c0 >= T else 0
            nb = n // 128
            x_t = xr.next(); sq = sqr.next(); rs = rsr.next(); h = hr.next()
            kb.dma("sp", x_t[:, :, :n], xs[:, :, c0:c0 + n].rearrange("k p n -> p k n"),
                   [kb.dbuf("xs", k) for k in range(8)], (x_t,), "ax0")
            cs = csr.next(); cm = cmr.next()
            kb.dma("sp", cs[:, 0, :n], g["cosW"][:, c0:c0 + n], (), (cs,), "acs0")
            kb.dma("sp", cs[:, 1, :n], g["sinW"][:, c0:c0 + n], (), (cs,), "acs0")
            kb.dma("sp", cm[0:96, 0, :n], g["cosM"][:, c0:c0 + n], (), (cm,), "acm0")
            kb.dma("sp", cm[0:96, 1, :n], g["sinM"][:, c0:c0 + n], (), (cm,), "acm0")
            kb.act(sq[:, :, :n], x_t[:, :, :n], AF.Square, (x_t,), (sq,))
            ps = psrot.next()
            kb.mm(ps[:, :n], [(onesb[:], sq[:, k, :n]) for k in range(8)], (onesb, sq), (ps,))
            rstd(kb, rs[:, :n], ps[:, :n], 1.0 / D, EPS, (ps,), (rs,))
            for k in range(8):
                tm = tmr.next()
                kb.tt("dve", tm[:, :n], x_t[:, k, :n], rs[:, :n], ALU.mult, (x_t, rs), (tm,))
                kb.act(h[:, k, :n], tm[:, :n], AF.Identity, (tm, modA, modB), (h,),
                       bias=modB[:, l, k, isc:isc + 1], scale=modA[:, l, k, isc:isc + 1])
            kb.dma("pool", hT[:, :, c0:c0 + n].rearrange("k p n -> p k n"), h[:, :, :n], (h,),
                   [kb.dbuf("hT", ti)], "ah%d" % (ti % 2))

            def proj(ps_ap, wt, col0, ncol, m0=0):
                kb.mm(ps_ap, [(wt[:, k, col0:col0 + ncol], h[:, k, :n]) for k in range(8)], (wt, h), (ps,))

            for j in range(13):
                ps = psrot.next()
                proj(ps[:, :n], W, O_RW + j * 128, 128)
                rw = rwr.next()
                kb.copy("act", rw[:, :n], ps[:, :n], (ps,), (rw,))
                kb.dma("pool", rwT[j, :, c0:c0 + n], rw[:, :n], (rw,), [kb.dbuf("rwT", ti)], "arw%d" % (j % 3))
            for j in range(4):
                ps = psrot.next()
                proj(ps[:, :n], W, O_GLU + 512 + j * 128, 128)
                sg = sgr.next()
                kb.act(sg[:, :n], ps[:, :n], AF.Sigmoid, (ps,), (sg,))
                ps = psrot.next()
                proj(ps[:, :n], W, O_GLU + j * 128, 128)
                u = ur.next()
                kb.tt("dve", u[:, :n], ps[:, :n], sg[:, :n], ALU.mult, (ps, sg), (u,))
                kb.dma("pool", uT[j, :, c0:c0 + n], u[:, :n], (u,), [kb.dbuf("uT", ti)], "au%d" % (j % 3))
            for j in range(5):
                col = (O_WQ + j * 128) if j < 4 else O_WK
                rcol = j * 128
                ps = psrot.next()
                proj(ps[:, :n], W, col, 128)
                t1 = tmr.next()
                kb.tt("dve", t1[:, :n], ps[:, :n], cs[:, 0, :n], ALU.mult, (ps, cs), (t1,))
                ps = psrot.next()
                proj(ps[:, :n], Wr, rcol, 128)
                t2 = tmr.next()
                kb.tt("dve", t2[:, :n], ps[:, :n], cs[:, 1, :n], ALU.mult, (ps, cs), (t2,))
                qo = qor.next()
                kb.tt("pool", qo[:, :n], t1[:, :n], t2[:, :n], ALU.add, (t1, t2), (qo,))
                if j < 4:
                    for hh in range(2):
                        kb.dma("pool", qwT[2 * j + hh, :, c0:c0 + n], qo[hh * 64:(hh + 1) * 64, :n], (qo,),
                               [kb.dbuf("qwT", ti)], "aqo%d" % (j % 4))
                else:
                    for hh in range(2):
                        kb.dma("pool", kwT[hh, :, c0:c0 + n], qo[hh * 64:(hh + 1) * 64, :n], (qo,),
                               [kb.dbuf("kwT", ti)], "aqo%d" % (j % 4))
            ps = psrot.next()
            for b in range(nb):
                kb.mm(ps[:, b * 128:(b + 1) * 128],
                      [(h[:, k, b * 128:(b + 1) * 128], W[:, k, O_WV:O_WV + 128]) for k in range(8)], (W, h), (ps,))
            vt = vwr.next()
            kb.copy("act", vt[:, :nb, :], ps[:, :nb * 128].rearrange("p (b c) -> p b c", b=nb), (ps,), (vt,))
            kb.dma("pool", vw[c0:c0 + n, :].rearrange("(b p) c -> p b c", p=128), vt[:, :nb, :], (vt,),
                   [kb.dbuf("vw", ti)], "avw%d" % (ti % 2))
            dq = dqr.next(); cq = cqr.next()
            for j in range(2):
                ps = psrot.next()
                proj(ps[:, :n], W, O_DQ + j * 128, 128)
                kb.copy("act", dq[:, j, :n], ps[:, :n], (ps,), (dq,))
            sq2 = sqr.next()
            kb.act(sq2[:, 0:2, :n], dq[:, :, :n], AF.Square, (dq,), (sq2,))
            ps = psrot.next()
            kb.mm(ps[:, :n], [(onesb[:], sq2[:, k, :n]) for k in range(2)], (onesb, sq2), (ps,))
            rs2 = rsr.next()
            rstd(kb, rs2[:, :n], ps[:, :n], 1.0 / 256, EPS, (ps,), (rs2,))
            for j in range(2):
                kb.stt("dve", cq[:, j, :n], dq[:, j, :n], vec[:, l, oq + j:oq + j + 1], rs2[:, :n], ALU.mult, ALU.mult,
                       (dq, rs2, vec), (cq,))
            for hh in range(8):
                ps = psrot.next()
                kb.mm(ps[0:96, :n], [(qup[:, k, hh * 96:(hh + 1) * 96], cq[:, k, :n]) for k in range(2)], (qup, cq), (ps,))
                t1 = tmr.next()
                kb.tt("dve", t1[0:96, :n], ps[0:96, :n], cm[0:96, 0, :n], ALU.mult, (ps, cm), (t1,))
                ps = psrot.next()
                kb.mm(ps[0:96, :n], [(qupr[:, k, hh * 96:(hh + 1) * 96], cq[:, k, :n]) for k in range(2)], (qupr, cq), (ps,))
                t2 = tmr.next()
                kb.tt("dve", t2[0:96, :n], ps[0:96, :n], cm[0:96, 1, :n], ALU.mult, (ps, cm), (t2,))
                qo = qor.next()
                kb.tt("pool", qo[0:96, :n], t1[0:96, :n], t2[0:96, :n], ALU.add, (t1, t2), (qo,))
                kb.dma("pool", qmT[hh, :, c0:c0 + n], qo[0:96, :n], (qo,), [kb.dbuf("qmT", ti)], "aqo%d" % (hh % 4))
            dk = dkr.next(); ck = ckr.next()
            ps = psrot.next()
            proj(ps[:, :n], W, O_DKV, 128)
            kb.copy("act", dk[:, :n], ps[:, :n], (ps,), (dk,))
            sq3 = sqr.next()
            kb.act(sq3[:, 0, :n], dk[:, :n], AF.Square, (dk,), (sq3,))
            ps = psrot.next()
            kb.mm(ps[:, :n], [(onesb[:], sq3[:, 0, :n])], (onesb, sq3), (ps,))
            rs3 = rsr.next()
            rstd(kb, rs3[:, :n], ps[:, :n], 1.0 / 128, EPS, (ps,), (rs3,))
            kb.stt("dve", ck[:, :n], dk[:, :n], vec[:, l, okv:okv + 1], rs3[:, :n], ALU.mult, ALU.mult,
                   (dk, rs3, vec), (ck,))
            psk = psrot.next()
            kb.mm(psk[64:96, :n], [(W[:, k, O_KR:O_KR + 32], h[:, k, :n]) for k in range(8)], (W, h), (psk,))
            t1 = tmr.next()
            kb.tt("dve", t1[64:96, :n], psk[64:96, :n], cm[64:96, 0, :n], ALU.mult, (psk, cm), (t1,))
            psk2 = psrot.next()
            kb.mm(psk2[64:96, :n], [(Wr[:, k, 640:672], h[:, k, :n]) for k in range(8)], (Wr, h), (psk2,))
            t2 = tmr.next()
            kb.tt("dve", t2[64:96, :n], psk2[64:96, :n], cm[64:96, 1, :n], ALU.mult, (psk2, cm), (t2,))
            krt = qor.next()
            kb.tt("pool", krt[64:96, :n], t1[64:96, :n], t2[64:96, :n], ALU.add, (t1, t2), (krt,))
            for hh in range(8):
                kb.dma("pool", kmT[hh, 64:96, c0:c0 + n], krt[64:96, :n], (krt,), [kb.dbuf("kmT", ti)], "akr")
            for hh in range(8):
                ps = psrot.next()
                kb.mm(ps[0:64, :n], [(kvk[:, hh, :], ck[:, :n])], (kvk, ck), (ps,))
                km = kmr.next()
                kb.copy("act", km[0:64, :n], ps[0:64, :n], (ps,), (km,))
                kb.dma("pool", kmT[hh, 0:64, c0:c0 + n], km[0:64, :n], (km,), [kb.dbuf("kmT", ti)], "akm%d" % (hh % 4))
            vmt = vmr.next()
            for b in range(nb):
                ps = psrot.next()
                kb.mm(ps[:, :], [(ck[:, b * 128:(b + 1) * 128], kvv[:].rearrange("p h d -> p (h d)"))], (kvv, ck), (ps,))
                kb.copy("act", vmt[:, b, :], ps[:, :], (ps,), (vmt,))
            kb.dma("pool", vm[c0:c0 + n, :].rearrange("(b p) c -> p b c", p=128), vmt[:, :nb, :], (vmt,),
                   [kb.dbuf("vm", ti)], "avm0")


def phase_RW(nc, kb, cfg, l, env):
    T, TT = cfg.T, cfg.TT
    NC, NPB = TT // 64, TT // 128
    NCL = T // 64
    g = env
    last = (l == cfg.depth - 1)
    psrot, vec, identb, ident, onesf = Rot(g["psb"][0:7]), g["vec"], g["identb"], g["ident"], g["onesf"]
    psOLbank = g["psb"][7]
    bdones, rmask, resetm = g["bdones"], g["rmask"], g["resetm"]
    sb, sbrot = g["sb"], g["sbrot"]
    rwT, yT = g["rwT"], g["yT"]
    McD, NcD, GD, QhD, OLD, bvD = g["McD"], g["NcD"], g["GD"], g["QhD"], g["OLD"], g["bvD"]
    omu, _ = VEC["mu"]; ow0, _ = VEC["w0"]; oa0, _ = VEC["a0"]; okk, _ = VEC["k_k"]; oka, _ = VEC["k_a"]; ork, _ = VEC["r_k"]
    ntl = len(cfg.tiles)
    rw_bufs = [kb.dbuf("rwT", j) for j in range(ntl)]
    with contextlib.ExitStack() as ph:
        etot = sb("retot", [128, 2, 4, NC], F32, ph)
        with contextlib.ExitStack() as st:
            lw = sb("rlw", [128, 2, 512], BF16, st)
            for d in range(2):
                kb.dma("pool", lw[0:64, d, :], g["w_up"][l, d, :, :], (), (lw,), "rc")
                kb.dma("pool", lw[64:128, d, :], g["a_up"][l, d, :, :], (), (lw,), "rc")
            omka = sb("romka", [128, 4], F32, st)
            kb.ts("dve", omka[:], vec[:, l, oka:oka + 4], -1.0, 1.0, ALU.mult, ALU.add, (vec,), (omka,))
            rtr = sbrot("rrt", [128, 13, 514], F32, 1, st)
            mr = sbrot("rm", [128, 13, 512], F32, 1, st)
            lor = sbrot("rlo", [128, 512], BF16, 1, st)
            vbr = sbrot("rvb", [128, 4, 512], BF16, 1, st)
            kkr_ = sbrot("rkk", [128, 4, 512], F32, 1, st)
            tmr = sbrot("rt", [128, 512], F32, 14, st)
            tbr = sbrot("rtb", [128, 512], BF16, 2, st)
            kdsr = sbrot("rkds", [128, 512], F32, 2, st)
            fmr = sbrot("rfm", [128, 32, 512], BF16, 1, st)
            scr = sbrot("rsc", [128, 512], BF16, 6, st)
            yr_ = sbrot("rY", [128, 512], BF16, 2, st)
            zlr = sbrot("rzl", [128, 6, 512], BF16, 1, st)
            xfr = sbrot("rxf", [128, 512], F32, 1, st)
            xbr = sbrot("rxb", [128, 512], BF16, 2, st)
            vtr = sbrot("rvt", [128, 512], BF16, 2, st)
            tm2r = sbrot("rtm2", [128, 384], BF16, 4, st)
            tmcr = sbrot("rtmc", [128, 2, 256], BF16, 2, st)
            cmask = sb("rcmask", [128, 2], F32, st)
            kb.memset("pool", cmask[:], 0.0, (cmask,))
            kb.memset("pool", cmask[0:64, 0:1], 1.0, (cmask,))
            kb.memset("pool", cmask[64:128, 1:2], 1.0, (cmask,))
            qhr = sbrot("rqh", [128, 2, 128], BF16, 2, st)
            for t_ in qhr.tiles:
                kb.memset("pool", t_[:], 0.0, (t_,))
            mtr = sbrot("rmt", [128, 2, 128], F32, 2, st)
            for t_ in mtr.tiles:
                kb.memset("pool", t_[:], 0.0, (t_,))
            ncr = sbrot("rnc", [128, 2, 64], F32, 2, st)
            olr = sbrot("rol", [128, 512], F32, 2, st)
            for ti, (c0, n) in enumerate(cfg.tiles):
                s0, s1 = (0, T) if c0 < T else (T, TT)
                lo_, hi_ = max(c0 - 1, s0), min(c0 + n + 1, s1)
                rt = rtr.next(); m = mr.next(); lo = lor.next(); vb = vbr.next(); kk = kkr_.next(); fm = fmr.next()
                if lo_ > c0 - 1 or hi_ < c0 + n + 1:
                    kb.memset("pool", rt[:], 0.0, (rt,))
                off = lo_ - (c0 - 1)
                kb.dma("sp", rt[:, :, off:off + hi_ - lo_], rwT[:, :, lo_:hi_].rearrange("j p n -> p j n"), rw_bufs, (rt,), "rrt")
                for j in range(13):
                    nb = tmr.next()
                    kb.tt("pool", nb[:, :n], rt[:, j, 0:n], rt[:, j, 2:n + 2], ALU.add, (rt,), (nb,))
                    kb.stt("dve", nb[:, :n], nb[:, :n], 0.5, rt[:, j, 1:n + 1], ALU.mult, ALU.subtract, (nb, rt), (nb,))
                    kb.stt("dve", m[:, j, :n], nb[:, :n], vec[:, l, omu + j:omu + j + 1], rt[:, j, 1:n + 1], ALU.mult, ALU.add,
                           (nb, rt, vec), (m,))
                kb.act(lo[0:64, :n], m[0:64, 12, :n], AF.Tanh, (m,), (lo,))
                kb.copy("act", lo[64:128, :n], m[64:128, 12, :n], (m,), (lo,))
                for cj in range(4):
                    kb.copy("pool", vb[:, cj, :n], m[:, 8 + cj, :n], (m,), (vb,))
                    kr_ = tmr.next()
                    kb.ts("dve", kr_[:, :n], m[:, 4 + cj, :n], vec[:, l, okk + cj:okk + cj + 1], None, ALU.mult, None, (m, vec), (kr_,))
                    sqb = tbr.next()
                    kb.act(sqb[:, :n], kr_[:, :n], AF.Square, (kr_,), (sqb,))
                    ps = psrot.next()
                    kb.mm(ps[:, :n], [(bdones[:], sqb[:, :n])], (bdones, sqb), (ps,))
                    rn = tmr.next()
                    kb.ts("dve", rn[:, :n], ps[:, :n], 1e-24, None, ALU.add, None, (ps,), (rn,))
                    kb.act(rn[:, :n], rn[:, :n], AF.Sqrt, (rn,), (rn,))
                    kb.op("dve", (lambda o: (lambda e: e.reciprocal(o, o)))(rn[:, :n]), (rn,), (rn,))
                    kb.tt("dve", kk[:, cj, :n], kr_[:, :n], rn[:, :n], ALU.mult, (kr_, rn), (kk,))
                    kds = kdsr.next()
                    for d in range(2):
                        ps = psrot.next()
                        kb.mm(ps[:, :n], [(lw[0:64, d, cj * 128:(cj + 1) * 128], lo[0:64, :n])], (lw, lo), (ps,))
                        lwd = tmr.next()
                        kb.act(lwd[:, :n], ps[:, :n], AF.Sigmoid, (ps, vec), (lwd,), bias=vec[:, l, ow0 + d * 4 + cj:ow0 + d * 4 + cj + 1])
                        kb.ts("pool", lwd[:, :n], lwd[:, :n], -0.6065306597126334, None, ALU.mult, None, (lwd,), (lwd,))
                        ps = psrot.next()
                        kb.mm(ps[:, :n], [(lw[64:128, d, cj * 128:(cj + 1) * 128], lo[64:128, :n])], (lw, lo), (ps,))
                        a_ = tmr.next()
                        kb.act(a_[:, :n], ps[:, :n], AF.Sigmoid, (ps, vec), (a_,), bias=vec[:, l, oa0 + d * 4 + cj:oa0 + d * 4 + cj + 1])
                        P = tmr.next(); Pm = tmr.next()
                        kb.op("dve", (lambda o, i0, i1: (lambda e: e.tensor_tensor_scan(o, i0, i1, 0.0, ALU.mult, ALU.add)))(
                            P[:, :n], resetm[:, :n], lwd[:, :n]), (resetm, lwd), (P,))
                        kb.tt("pool", Pm[:, :n], P[:, :n], lwd[:, :n], ALU.subtract, (P, lwd), (Pm,))
                        E1 = tmr.next(); E2 = tmr.next(); E3 = tmr.next(); E4 = tmr.next()
                        kb.act(E1[:, :n], P[:, :n], AF.Exp, (P,), (E1,))
                        kb.act(E2[:, :n], P[:, :n], AF.Exp, (P,), (E2,), scale=-1.0)
                        kb.act(E3[:, :n], Pm[:, :n], AF.Exp, (Pm,), (E3,))
                        kb.act(E4[:, :n], Pm[:, :n], AF.Exp, (Pm,), (E4,), scale=-1.0)
                        kb.copy("pool", etot[:, d, cj, c0 // 64:(c0 + n) // 64], E1[:, 63:n:64], (E1,), (etot,))
                        kd = tmr.next(); b_ = tmr.next()
                        kb.ts("dve", kd[:, :n], a_[:, :n], vec[:, l, oka + cj:oka + cj + 1], omka[:, cj:cj + 1], ALU.mult, ALU.add,
                              (a_, vec, omka), (kd,))
                        kb.tt("dve", kd[:, :n], kd[:, :n], m[:, 4 + cj, :n], ALU.mult, (kd, m), (kd,))
                        kb.tt("pool", b_[:, :n], kk[:, cj, :n], a_[:, :n], ALU.mult, (kk, a_), (b_,))
                        er, ek, ekk = (E1, E2, E3) if d == 0 else (E4, E3, E2)
                        kb.tt("dve", fm[:, (d * 4 + cj) * 4 + 0, :n], m[:, cj, :n], er[:, :n], ALU.mult, (m, er), (fm,))
                        kb.tt("pool", fm[:, (d * 4 + cj) * 4 + 1, :n], kd[:, :n], ek[:, :n], ALU.mult, (kd, ek), (fm,))
                        kb.stt("dve", fm[:, (d * 4 + cj) * 4 + 2, :n], b_[:, :n], -1.0, ek[:, :n], ALU.mult, ALU.mult, (b_, ek), (fm,))
                        kb.tt("pool", fm[:, (d * 4 + cj) * 4 + 3, :n], kk[:, cj, :n], ekk[:, :n], ALU.mult, (kk, ekk), (fm,))
                        if d == 0:
                            kb.copy("pool", kds[:, :n], kd[:, :n], (kd,), (kds,))
                        else:
                            kb.tt("pool", kds[:, :n], kds[:, :n], kd[:, :n], ALU.add, (kds, kd), (kds,))
                    prod = tbr.next()
                    kb.stt("dve", prod[:, :n], m[:, cj, :n], vec[:, l, ork + cj:ork + cj + 1], kds[:, :n], ALU.mult, ALU.mult,
                           (m, vec, kds), (prod,))
                    ps = psrot.next()
                    kb.mm(ps[:, :n], [(bdones[:], prod[:, :n])], (bdones, prod), (ps,))
                    bv = tmr.next()
                    kb.tt("dve", bv[:, :n], ps[:, :n], m[:, 8 + cj, :n], ALU.mult, (ps, m), (bv,))
                    kb.dma("pool", bvD[cj, :, c0:c0 + n], bv[:, :n], (bv,), [kb.dbuf("bvD", ti)], "rbv")
                if cfg.debug and ti == 0:
                    dfm = nc.dram_tensor("dbg_fm%d" % l, [128, 32, 512], BF16, kind="ExternalOutput").ap()
                    dm = nc.dram_tensor("dbg_m%d" % l, [128, 13, 512], F32, kind="ExternalOutput").ap()
                    dkk = nc.dram_tensor("dbg_kk%d" % l, [128, 4, 512], F32, kind="ExternalOutput").ap()
                    kb.dma("pool", dfm[:, :, :], fm[:], (fm,), [kb.dbuf("dbgfm")], "rdbg")
                    kb.dma("pool", dm[:, :, :], m[:], (m,), [kb.dbuf("dbgm2")], "rdbg")
                    kb.dma("pool", dkk[:, :, :], kk[:], (kk,), [kb.dbuf("dbgkk")], "rdbg")
                for pb in range(n // 128 if RWSTOP >= 2 else 0):
                    cs_ = slice(pb * 128, (pb + 1) * 128)
                    gpb = (c0 + pb * 128) // 128
                    gc0 = 2 * gpb
                    psv = psrot.next()
                    for cj in range(4 if RWZ >= 1 else 0):
                        kb.mm(psv[:, cj * 128:(cj + 1) * 128], [(vb[:, cj, cs_], identb[:])], (vb, identb), (psv,))
                    vtm = vtr.next()
                    kb.copy("act", vtm[:], psv[:], (psv,), (vtm,))
                    ol = olr.next()
                    for d in range(2):
                        psOL = psOLbank
                        for g4 in range(2):
                            sc = []
                            opnd = []
                            for u in range(4):
                                h = 4 * g4 + u; cj = h // 2; e = h % 2; b0 = 64 * e
                                rt_ = fm[b0:b0 + 64, (d * 4 + cj) * 4 + 0, cs_]; kt_ = fm[b0:b0 + 64, (d * 4 + cj) * 4 + 1, cs_]
                                pt_ = fm[b0:b0 + 64, (d * 4 + cj) * 4 + 2, cs_]; kkt_ = fm[b0:b0 + 64, (d * 4 + cj) * 4 + 3, cs_]
                                opnd.append((h, cj, e, b0, rt_, kt_, pt_, kkt_))
                                if RWY < 1:
                                    continue
                                ps1 = psrot.next()
                                kb.mm(ps1[:, 0:128], [(pt_, kkt_)], (fm,), (ps1,))
                                kb.mm(ps1[:, 128:256], [(pt_, rt_)], (fm,), (ps1,))
                                kb.mm(ps1[:, 256:384], [(kt_, kkt_)], (fm,), (ps1,))
                                kb.mm(ps1[:, 384:512], [(kt_, rt_)], (fm,), (ps1,))
                                s4 = scr.next()
                                kb.tt("dve", s4[:], ps1[:], rmask[:, d, :], ALU.mult, (ps1, rmask), (s4,))
                                sc.append(s4)
                            psA4 = psrot.next()
                            for u in range(4):
                                kb.mm(psA4[:, u * 128:(u + 1) * 128], [(sc[u][:, 0:128], identb[:])], (sc[u], identb), (psA4,))
                            Y = yr_.next()
                            kb.copy("dve", Y[:], psA4[:], (psA4,), (Y,))
                            zl = zlr.next()
                            if RWX < 2:
                                continue

                            def Zn(nlev, u):
                                if nlev == 0:
                                    return sc[u][:, 0:128], sc[u]
                                return zl[:, nlev, u * 128:(u + 1) * 128], zl
                            for nlev in range(5):
                                psZ = psrot.next()
                                for u in range(4):
                                    z_ap, z_t = Zn(nlev, u)
                                    kb.mm(psZ[:, u * 128:(u + 1) * 128], [(Y[:, u * 128:(u + 1) * 128], z_ap)], (Y, z_t), (psZ,))
                                if nlev < 4:
                                    psY = psrot.next()
                                    for u in range(4):
                                        z_ap, z_t = Zn(nlev, u)
                                        kb.mm(psY[:, u * 128:(u + 1) * 128], [(z_ap, Y[:, u * 128:(u + 1) * 128])], (Y, z_t), (psY,))
                                kb.copy("act", zl[:, nlev + 1, :], psZ[:], (psZ,), (zl,))
                                if nlev < 4:
                                    Y = yr_.next()
                                    kb.copy("dve", Y[:], psY[:], (psY,), (Y,))
                            if RWX < 3:
                                continue
                            tms = []
                            for pr_ in range(2):
                                cj = 2 * g4 + pr_
                                psT = psrot.next()
                                kb.mm(psT[:, 0:128], [(fm[:, (d * 4 + cj) * 4 + 2, cs_], identb[:])], (fm, identb), (psT,))
                                kb.mm(psT[:, 128:256], [(fm[:, (d * 4 + cj) * 4 + 1, cs_], identb[:])], (fm, identb), (psT,))
                                kb.mm(psT[:, 256:384], [(fm[:, (d * 4 + cj) * 4 + 3, cs_], identb[:])], (fm, identb), (psT,))
                                tm2 = tm2r.next()
                                kb.copy("act", tm2[:], psT[:, 0:384], (psT,), (tm2,))
                                tms.append(tm2)
                            psX = psrot.next()
                            for u in range(4):
                                h, cj, e, b0, rt_, kt_, pt_, kkt_ = opnd[u]
                                kb.mm(psX[:, u * 64:(u + 1) * 64], [(sc[u][:, 256:384], vtm[:, h * 64:(h + 1) * 64])], (sc[u], vtm), (psX,))
                            xf = xfr.next(); xb = xbr.next()
                            xf3 = xf[:].rearrange("p (u w) -> p u w", u=4)
                            kb.copy("dve", xf3[:, :, 0:64], psX[:, 0:256].rearrange("p (u w) -> p u w", u=4), (psX,), (xf,))
                            for pr_ in range(2):
                                kb.copy("pool", xf3[:, 2 * pr_:2 * pr_ + 2, 64:128],
                                        tms[pr_][:, 256:384].rearrange("p (e w) -> p e w", e=2), (tms[pr_],), (xf,))
                            kb.copy("act", xb[:], xf[:], (xf,), (xb,))
                            for nlev in range(6):
                                psL = psrot.next()
                                for u in range(4):
                                    z_ap, z_t = Zn(nlev, u)
                                    kb.mm(psL[:, u * 128:(u + 1) * 128], [(z_ap, xb[:, u * 128:(u + 1) * 128])], (z_t, xb), (psL,))
                                kb.tt("dve", xf[:], xf[:], psL[:], ALU.add, (xf, psL), (xf,))
                                xb = xbr.next()
                                kb.copy("act", xb[:], xf[:], (xf,), (xb,))
                            for pr_ in range(2 if RWX >= 4 else 0):
                                cj = 2 * g4 + pr_
                                tm2 = tms[pr_]
                                tmc = tmcr.next()
                                for c in range(2):
                                    kb.ts("pool", tmc[:, c, :], tm2[:, 0:256], cmask[:, c:c + 1], None, ALU.mult, None, (tm2, cmask), (tmc,))
                                qh = qhr.next(); mt = mtr.next(); ncs = ncr.next()
                                psQ = psrot.next(); psN = psrot.next(); psM = psrot.next()
                                for e in range(2):
                                    u = 2 * pr_ + e
                                    h, cj_, e_, b0, rt_, kt_, pt_, kkt_ = opnd[u]
                                    W1 = xb[:, u * 128:u * 128 + 64]; W2 = xb[:, u * 128 + 64:u * 128 + 128]
                                    kb.mm(psQ[b0:b0 + 64, 0:128], [(W2, sc[u][:, 128:256])], (xb, sc[u]), (psQ,))
                                    for c in range(2):
                                        kb.tt("dve", qh[b0:b0 + 64, c, c * 64:(c + 1) * 64], psQ[b0:b0 + 64, c * 64:(c + 1) * 64],
                                              fm[b0:b0 + 64, (d * 4 + cj) * 4 + 0, pb * 128 + c * 64:pb * 128 + (c + 1) * 64], ALU.add, (psQ, fm), (qh,))
                                    kb.mm(psOL[:, h * 64:(h + 1) * 64],
                                          [(sc[u][:, 128:256], W1), (sc[u][:, 384:512], vtm[:, h * 64:(h + 1) * 64])], (sc[u], xb, vtm), (psOL,))
                                    for c in range(2 if RWX >= 6 else 0):
                                        kb.mm(psN[b0:b0 + 64, c * 64:(c + 1) * 64],
                                              [(tmc[:, c, e * 64:(e + 1) * 64], xb[:, u * 128:u * 128 + 64]),
                                               (tmc[:, c, 128 + e * 64:128 + (e + 1) * 64], vtm[:, h * 64:(h + 1) * 64])],
                                              (tmc, xb, vtm), (psN,))
                                        kb.mm(psM[b0:b0 + 64, c * 128 + b0:c * 128 + b0 + 64],
                                              [(xb[:, u * 128 + 64:u * 128 + 128], tmc[:, c, e * 64:(e + 1) * 64])], (xb, tmc), (psM,))
                                kb.dma("pool", QhD[d, cj, gpb, :, :], qh[:].rearrange("p c n -> p (c n)"), (qh,), [kb.dbuf("QhD", gpb)], "rqh%d" % (qhr.i % 2))
                                kb.copy("act", ncs[:].rearrange("p c v -> p (c v)"), psN[:, 0:128], (psN,), (ncs,))
                                kb.dma("pool", NcD[d, cj, gc0:gc0 + 2, :, :].rearrange("c p v -> p c v"), ncs[:], (ncs,), [kb.dbuf("NcD", gpb)], "rnc%d" % (ncr.i % 2))
                                for e in range(2):
                                    b0 = 64 * e
                                    kb.copy("dve", mt[b0:b0 + 64, :, b0:b0 + 64],
                                            psM[b0:b0 + 64, 0:256].rearrange("p (c k) -> p c k", c=2)[:, :, b0:b0 + 64], (psM,), (mt,))
                                kb.dma("pool", McD[d, cj, gc0:gc0 + 2, :, :].rearrange("c p k -> p c k"), mt[:], (mt,), [kb.dbuf("McD", gpb)], "rmt%d" % (mtr.i % 2))
                        if d == 0:
                            kb.copy("dve", ol[:], psOL[:], (psOL,), (ol,))
                        else:
                            kb.tt("dve", ol[:], ol[:], psOL[:], ALU.add, (ol, psOL), (ol,))
                    kb.dma("pool", OLD[gpb, :, :], ol[:], (ol,), [kb.dbuf("OLD", gpb)], "rol%d" % (olr.i % 2))
        kb.flush()
        with contextlib.ExitStack() as st:
            order = [list(range(NCL, NC)) + list(range(NCL)), list(range(NC - 1, NCL - 1, -1)) + list(range(NCL - 1, -1, -1))]
            gr = [[sbrot("rG%d%d" % (d, cj), [128, 64], F32, 2, st) for cj in range(4)] for d in range(2)]
            gbr = sbrot("rGb", [128, 64], BF16, 4, st)
            mcr = sbrot("rmc", [128, 128], F32, 6, st)
            nlr = sbrot("rnl", [128, 64], F32, 6, st)
            ncer = sbrot("rnce", [128, 64], F32, 6, st)
            allmc = [kb.dbuf("McD", j) for j in range(NPB)]
            allnc = [kb.dbuf("NcD", j) for j in range(NPB)]
            gcur = [[None] * 4 for _ in range(2)]
            for d in range(2):
                for cj in range(4):
                    t_ = gr[d][cj].next()
                    kb.memset("pool", t_[:], 0.0, (t_,))
                    gcur[d][cj] = t_
            for s_ in range(NC if RWSTOP >= 3 else 0):
                for d in range(2):
                    for cj in range(4):
                        c = order[d][s_]
                        nxt = order[d][s_ + 1] if s_ + 1 < NC else None
                        G = gcur[d][cj]
                        gb = gbr.next()
                        kb.copy("pool", gb[:], G[:], (G,), (gb,))
                        kb.dma("pool", GD[d, cj, c, :, :], gb[:], (gb,), [kb.dbuf("GD", c // 2)], "rgb%d" % (gbr.i % 4))
                        if nxt is None:
                            continue
                        mc = mcr.next(); nl = nlr.next(); nce = ncer.next()
                        kb.dma("sp", mc[:], McD[d, cj, c, :, :], allmc, (mc,), "rmc%d" % (mcr.i % 6))
                        kb.dma("sp", nl[:], NcD[d, cj, c, :, :], allnc, (nl,), "rnl%d" % (nlr.i % 6))
                        E = etot[:, d, cj, c:c + 1] if d == 0 else etot[:, d, cj, nxt:nxt + 1]
                        kb.ts("pool", nce[:], nl[:], E, None, ALU.mult, None, (nl, etot), (nce,))
                        ps = psrot.next()
                        kb.mm(ps[:, 0:64], [(mc[:], G[:]), (ident[:], G[:])], (mc, G, ident), (ps,))
                        Gn = gr[d][cj].next()
                        kb.stt("dve", Gn[:], ps[:, 0:64], E, nce[:], ALU.mult, ALU.add, (ps, etot, nce), (Gn,))
                        gcur[d][cj] = Gn
        kb.flush()
        with contextlib.ExitStack() as st:
            gn = sb("rgn", [128, 2, 512], F32, st)
            kb.dma("sp", gn[:, 0, :], g["gnbc"][l, 0, :, :], (), (gn,), "rc")
            kb.dma("sp", gn[:, 1, :], g["gnbc"][l, 1, :, :], (), (gn,), "rc")
            qar = sbrot("rqa", [128, 8, 256], BF16, 2, st)
            gar = sbrot("rga", [128, 8, 2, 128], BF16, 2, st)
            for t_ in gar.tiles:
                kb.memset("pool", t_[:], 0.0, (t_,))
            olr = sbrot("rol2", [128, 512], F32, 2, st)
            bvr = sbrot("rbv2", [128, 4, 128], F32, 2, st)
            o_r = sbrot("ro", [128, 512], F32, 2, st)
            sqr = sbrot("rsq", [128, 512], F32, 2, st)
            str_ = sbrot("rst", [128, 4, 8], F32, 2, st)
            onr = sbrot("ron", [128, 512], F32, 2, st)
            obr = sbrot("rob", [128, 512], BF16, 2, st)
            yar = sbrot("rya", [128, 4, 128], BF16, 2, st)
            allq = [kb.dbuf("QhD", j) for j in range(NPB)]
            allg = [kb.dbuf("GD", j) for j in range(NPB)]
            npb_out = (T // 128) if last else NPB
            for gpb in range(npb_out if RWSTOP >= 4 else 0):
                gc0 = 2 * gpb
                t0 = gpb * 128
                qa = qar.next(); ga = gar.next(); ol = olr.next(); bv = bvr.next()
                kb.dma("sp", qa[:], QhD[:, :, gpb, :, :].rearrange("d j p x -> p (d j) x"), allq, (qa,), "rqa%d" % (gpb % 2))
                for d in range(2):
                    for cj in range(4):
                        for e in range(2):
                            b0 = 64 * e
                            kb.dma("sp", ga[b0:b0 + 64, d * 4 + cj, :, b0:b0 + 64],
                                   GD[d, cj, gc0:gc0 + 2, b0:b0 + 64, :].rearrange("c p v -> p c v"), allg, (ga,), "rga%d" % (gpb % 2))
                kb.dma("sp", ol[:], OLD[gpb, :, :], [kb.dbuf("OLD", gpb)], (ol,), "rol2%d" % (gpb % 2))
                kb.dma("sp", bv[:], bvD[:, :, t0:t0 + 128].rearrange("j p n -> p j n"), [kb.dbuf("bvD", j) for j in range(ntl)], (bv,),
                       "rbv2%d" % (gpb % 2))
                psO = psrot.next()
                for cj in range(4):
                    kb.mm(psO[:, cj * 128:(cj + 1) * 128],
                          [(qa[:, d * 4 + cj, c * 128:(c + 1) * 128], ga[:, d * 4 + cj, c, :]) for d in range(2) for c in range(2)],
                          (qa, ga), (psO,))
                o = o_r.next(); sq = sqr.next(); stt_ = str_.next(); on = onr.next(); ob = obr.next()
                kb.tt("dve", o[:], psO[:], ol[:], ALU.add, (psO, ol), (o,))
                kb.act(sq[:], o[:], AF.Square, (o,), (sq,))
                kb.op("dve", (lambda o_, i_: (lambda e_: e_.tensor_reduce(o_, i_, AX.X, ALU.add)))(
                    stt_[:, 0, :], o[:].rearrange("p (h v) -> p h v", h=8)), (o,), (stt_,))
                kb.op("dve", (lambda o_, i_: (lambda e_: e_.tensor_reduce(o_, i_, AX.X, ALU.add)))(
                    stt_[:, 1, :], sq[:].rearrange("p (h v) -> p h v", h=8)), (sq,), (stt_,))
                kb.ts("dve", stt_[:, 0, :], stt_[:, 0, :], 1.0 / 64, None, ALU.mult, None, (stt_,), (stt_,))
                kb.tt("dve", stt_[:, 2, :], stt_[:, 0, :], stt_[:, 0, :], ALU.mult, (stt_,), (stt_,))
                kb.stt("dve", stt_[:, 3, :], stt_[:, 1, :], 1.0 / 64, stt_[:, 2, :], ALU.mult, ALU.subtract, (stt_,), (stt_,))
                kb.ts("dve", stt_[:, 3, :], stt_[:, 3, :], 64e-5, None, ALU.add, None, (stt_,), (stt_,))
                kb.act(stt_[:, 3, :], stt_[:, 3, :], AF.Sqrt, (stt_,), (stt_,))
                kb.op("dve", (lambda o_: (lambda e_: e_.reciprocal(o_, o_)))(stt_[:, 3, :]), (stt_,), (stt_,))
                for h in range(8):
                    kb.ts("pool" if h % 2 else "dve", on[:, h * 64:(h + 1) * 64], o[:, h * 64:(h + 1) * 64], stt_[:, 0, h:h + 1],
                          stt_[:, 3, h:h + 1], ALU.subtract, ALU.mult, (o, stt_), (on,))
                kb.tt("pool", on[:], on[:], gn[:, 0, :], ALU.mult, (on, gn), (on,))
                kb.tt("pool", ob[:], on[:], gn[:, 1, :], ALU.add, (on, gn), (ob,))
                psT = psrot.next()
                for cj in range(4):
                    kb.mm(psT[:, cj * 128:(cj + 1) * 128], [(ob[:, cj * 128:(cj + 1) * 128], identb[:])], (ob, identb), (psT,))
                ya = yar.next()
                kb.tt("dve", ya[:].rearrange("p j n -> p (j n)"), psT[:], bv[:].rearrange("p j n -> p (j n)"), ALU.add, (psT, bv), (ya,))
                kb.dma("pool", yT[0, :, :, t0:t0 + 128].rearrange("j p n -> p j n"), ya[:], (ya,), [kb.dbuf("yT", 0, t0 // 512)],
                       "rya%d" % (gpb % 2))
        kb.flush()


def phase_conv(nc, kb, cfg, l, env):
    T, TT = cfg.T, cfg.TT
    g = env
    last = (l == cfg.depth - 1)
    psrot, vec, identb, onesf = g["psrot"], g["vec"], g["identb"], g["onesf"]
    sb, sbrot = g["sb"], g["sbrot"]
    uT, yT = g["uT"], g["yT"]
    ocw, _ = VEC["conv_w"]; ocb, _ = VEC["conv_b"]; olw, _ = VEC["ln_w"]; olb, _ = VEC["ln_b"]
    with contextlib.ExitStack() as st:
        dg = sb("cdg", [128, 124, 128], BF16, st)
        for i in range(124):
            kb.ts("pool", dg[:, i, :], identb[:], vec[:, l, ocw + i:ocw + i + 1], None, ALU.mult, None,
                  (identb, vec), (dg,))
        utr = sbrot("cu", [128, 4, 542], BF16, 2, st)
        cvr = sbrot("ccv", [128, 4, 512], F32, 2, st)
        sqr = sbrot("csq", [128, 4, 512], F32, 1, st)
        mr = sbrot("cm", [128, 512], F32, 2, st)
        vr = sbrot("cvv", [128, 512], F32, 2, st)
        tr = sbrot("ct", [128, 512], F32, 3, st)
        yr = sbrot("cy", [128, 512], BF16, 3, st)
        tl = cfg.tiles[:-1] if last else cfg.tiles
        for ti, (c0, n) in enumerate(tl):
            s0, s1 = (0, T) if c0 < T else (T, TT)
            lo, hi = max(c0 - 15, s0), min(c0 + n + 15, s1)
            ut = utr.next(); cv = cvr.next(); sq = sqr.next()
            if lo > c0 - 15 or hi < c0 + n + 15:
                kb.memset("pool", ut[:], 0.0, (ut,))
            off = lo - (c0 - 15)
            kb.dma("sp", ut[:, :, off:off + hi - lo], uT[:, :, lo:hi].rearrange("k p n -> p k n"),
                   [kb.dbuf("uT", j) for j in range(len(cfg.tiles))], (ut,), "cu%d" % (ti % 2))
            for ch in range(4):
                ps = psrot.next()
                kb.mm(ps[:, :n], [(dg[:, kk * 4 + ch, :], ut[:, ch, kk:kk + n]) for kk in range(31)], (dg, ut), (ps,))
                kb.act(cv[:, ch, :n], ps[:, :n], AF.Identity, (ps, vec), (cv,), bias=vec[:, l, ocb + ch:ocb + ch + 1])
            kb.act(sq[:, :, :n], cv[:, :, :n], AF.Square, (cv,), (sq,))
            ps1 = psrot.next()
            kb.mm(ps1[:, :n], [(onesf[:], cv[:, ch, :n]) for ch in range(4)], (onesf, cv), (ps1,))
            ps2 = psrot.next()
            kb.mm(ps2[:, :n], [(onesf[:], sq[:, ch, :n]) for ch in range(4)], (onesf, sq), (ps2,))
            mean = mr.next(); var = vr.next()
            kb.ts("dve", mean[:, :n], ps1[:, :n], 1.0 / 512, None, ALU.mult, None, (ps1,), (mean,))
            m2 = tr.next()
            kb.tt("dve", m2[:, :n], mean[:, :n], mean[:, :n], ALU.mult, (mean,), (m2,))
            kb.stt("dve", var[:, :n], ps2[:, :n], 1.0 / 512, m2[:, :n], ALU.mult, ALU.subtract, (ps2, m2), (var,))
            kb.ts("dve", var[:, :n], var[:, :n], 1e-5, None, ALU.add, None, (var,), (var,))
            kb.act(var[:, :n], var[:, :n], AF.Sqrt, (var,), (var,))
            kb.op("dve", (lambda o: (lambda e: e.reciprocal(o, o)))(var[:, :n]), (var,), (var,))
            for ch in range(4):
                t1 = tr.next()
                kb.tt("dve", t1[:, :n], cv[:, ch, :n], mean[:, :n], ALU.subtract, (cv, mean), (t1,))
                kb.tt("pool", t1[:, :n], t1[:, :n], var[:, :n], ALU.mult, (t1, var), (t1,))
                y = yr.next()
                kb.act(y[:, :n], t1[:, :n], AF.Silu, (t1, vec), (y,), bias=vec[:, l, olb + ch:olb + ch + 1],
                       scale=vec[:, l, olw + ch:olw + ch + 1])
                kb.dma("pool", yT[1, ch, :, c0:c0 + n], y[:, :n], (y,), [kb.dbuf("yT", 1, ti)], "cy%d" % (ch % 3))


def attn_finish(kb, g, pso, n, o_ap, o_tile, extra_row=None):
    dr = g["denr"].next(); rv = g["rinvr"].next(); psb = g["psB"].next(); onesf = g["onesf"]
    if extra_row is None:
        kb.copy("act", dr[64:65, :n], pso[64:65, :n], (pso,), (dr,))
    else:
        kb.tt("dve", dr[64:65, :n], pso[64:65, :n], extra_row[0], ALU.add, (pso, extra_row[1]), (dr,))
    kb.mm(psb[0:64, :n], [(onesf[64:65, 0:64], dr[64:65, :n])], (onesf, dr), (psb,))
    kb.op("dve", lambda e: e.reciprocal(rv[0:64, :n], psb[0:64, :n]), (psb,), (rv,))
    kb.tt("dve", o_ap, pso[0:64, :n], rv[0:64, :n], ALU.mult, (pso, rv), (o_tile,))


def phase_WA(nc, kb, cfg, l, env):
    T, TT = cfg.T, cfg.TT
    g = dict(env)
    last = (l == cfg.depth - 1)
    sb, sbrot = g["sb"], g["sbrot"]
    psb8 = g["psb"]
    qwT, kwT, vw, yT, wmask = g["qwT"], g["kwT"], g["vw"], g["yT"], g["wmask"]
    SC = 64 ** -0.5
    with contextlib.ExitStack() as st:
        psA = Rot(psb8[0:4]); psO = Rot(psb8[4:6]); g["psB"] = Rot(psb8[6:8])
        g["denr"] = sbrot("wden", [128, 512], F32, 2, st)
        g["rinvr"] = sbrot("wri", [128, 512], F32, 2, st)
        sink = sb("wsink", [128, 2, 512], F32, st)
        kc = sb("wkc", [128, 2, 256], BF16, st)
        vc = sb("wvc", [128, 2, 2, 65], BF16, st)
        kb.memset("dve", vc[:], 1.0, (vc,))
        for gg in range(2):
            kb.dma("sp", sink[64:65, gg, :], g["sink_in"][l, gg, :, :], (), (sink,), "wc")
            kb.dma("sp", kc[0:64, gg, :], kwT[gg, :, T:TT], [kb.dbuf("kwT", len(cfg.tiles) - 1)], (kc,), "wc")
        kb.act(sink[64:65, :, :], sink[64:65, :, :], AF.Exp, (sink,), (sink,))
        for cb in range(2):
            kb.dma("sp", vc[:, cb, :, 0:64], vw[T + cb * 128:T + cb * 128 + 128, :].rearrange("p (g d) -> p g d", g=2),
                   [kb.dbuf("vw", len(cfg.tiles) - 1)], (vc,), "wc")
        qr = sbrot("wq", [128, 4, 128], BF16, 2, st)
        kr = sbrot("wk", [128, 384], BF16, 2, st)
        vr = sbrot("wv", [128, 3, 2, 65], BF16, 2, st)
        for t_ in vr.tiles:
            kb.memset("dve", t_[:], 1.0, (t_,))
        pr = sbrot("wp", [128, 512], BF16, 4, st)
        orr = sbrot("wo", [128, 4, 128], BF16, 2, st)
        alltiles = [kb.dbuf(nm, j) for j in range(len(cfg.tiles)) for nm in ("qwT", "kwT", "vw")]
        nq = T // 128 + (0 if last else 2)
        for qb in range(nq):
            isctx = qb >= T // 128
            q0 = qb * 128
            if not isctx:
                blks = [b for b in (qb - 1, qb, qb + 1) if 0 <= b < T // 128]
                lo, hi = blks[0] * 128, blks[-1] * 128 + 128
                vt = vr.next()
                for bi, b_ in enumerate(blks):
                    kb.dma("sp", vt[:, bi, :, 0:64], vw[b_ * 128:b_ * 128 + 128, :].rearrange("p (g d) -> p g d", g=2),
                           alltiles, (vt,), "wv%d" % (qb % 2))
            for gg in range(2):
                q = qr.next()
                kb.dma("sp", q[0:64, :, :], qwT[4 * gg:4 * gg + 4, :, q0:q0 + 128].rearrange("h d n -> d h n"),
                       alltiles, (q,), "wq%d" % ((qb * 2 + gg) % 2))
                qf = q[0:64, :, :].rearrange("p h n -> p (h n)")
                keys = []
                if not isctx:
                    kt = kr.next()
                    kb.dma("sp", kt[0:64, 0:hi - lo], kwT[gg, :, lo:hi], alltiles, (kt,), "wk%d" % ((qb * 2 + gg) % 2))
                    for bi, b in enumerate(blks):
                        mk = 0 if b == qb - 1 else (1 if b == qb + 1 else None)
                        keys.append((kt[0:64, bi * 128:(bi + 1) * 128], vt[:, bi, gg, :], mk, (kt, vt)))
                for cb in range(2):
                    keys.append((kc[0:64, gg, cb * 128:(cb + 1) * 128], vc[:, cb, gg, :], None, (kc, vc)))
                pso = psO.next()
                for ki, (kT_ap, v_ap, mk, rb) in enumerate(keys):
                    ps = psA.next()
                    kb.mm(ps[:, :], [(kT_ap, qf)], (rb[0], q), (ps,))
                    p = pr.next()
                    kb.act(p[:, :], ps[:, :], AF.Exp, (ps,), (p,), scale=SC)
                    if mk is not None:
                        kb.tt("pool", p[:, :], p[:, :], wmask[:, mk, :], ALU.mult, (p, wmask), (p,))
                    kb.mm(pso[0:65, :], [(v_ap, p[:, :])], (rb[1], p), (pso,), first=(ki == 0), last=(ki == len(keys) - 1))
                o = orr.next()
                attn_finish(kb, g, pso, 512, o[0:64, :, :].rearrange("p h n -> p (h n)"), o,
                            extra_row=(sink[64:65, gg, :], sink))
                for hh in range(4):
                    kb.dma("pool", yT[2, 2 * gg + hh // 2, (hh % 2) * 64:(hh % 2) * 64 + 64, q0:q0 + 128], o[0:64, hh, :],
                           (o,), [kb.dbuf("yT", 2, q0 // 512)], "wo%d" % ((qb * 2 + gg) % 2))


def phase_MLA(nc, kb, cfg, l, env):
    T, TT = cfg.T, cfg.TT
    g = dict(env)
    last = (l == cfg.depth - 1)
    sb, sbrot = g["sb"], g["sbrot"]
    psb8 = g["psb"]
    qmT, kmT, vm, yT = g["qmT"], g["kmT"], g["vm"], g["yT"]
    SC = 96 ** -0.5
    NB = TT // 128
    with contextlib.ExitStack() as st:
        psA = Rot(psb8[0:4]); psO = Rot(psb8[4:6]); g["psB"] = Rot(psb8[6:8])
        g["denr"] = sbrot("dden", [128, 512], F32, 2, st)
        g["rinvr"] = sbrot("dri", [128, 512], F32, 2, st)
        ktr = sbrot("dk", [128, TT], BF16, 2, st)
        vtr = sbrot("dv", [128, NB, 65], BF16, 2, st)
        for t_ in vtr.tiles:
            kb.memset("dve", t_[:], 1.0, (t_,))
        qr = sbrot("dq", [128, 512], BF16, 2, st)
        pr = sbrot("dp", [128, 512], BF16, 4, st)
        orr = sbrot("do", [128, 512], BF16, 2, st)
        alltiles = [kb.dbuf(nm, j) for j in range(len(cfg.tiles)) for nm in ("qmT", "kmT", "vm")]
        tl = cfg.tiles[:-1] if last else cfg.tiles
        for h in range(8):
            kt = ktr.next(); vt = vtr.next()
            kb.dma("sp", kt[0:96, :], kmT[h, :, :], alltiles, (kt,), "dk%d" % (h % 2))
            for b8 in range(0, NB, 6):
                nb8 = min(6, NB - b8)
                kb.dma("sp", vt[:, b8:b8 + nb8, 0:64],
                       vm[b8 * 128:(b8 + nb8) * 128, h * 64:(h + 1) * 64].rearrange("(b p) d -> p b d", p=128), alltiles, (vt,),
                       "dv%d" % (h % 2))
            for ti, (c0, n) in enumerate(tl):
                q = qr.next()
                kb.dma("sp", q[0:96, :n], qmT[h, :, c0:c0 + n], alltiles, (q,), "dq%d" % (qr.i % 2))
                kblocks = range(NB) if c0 < T else range(T // 128, NB)
                pso = psO.next()
                nk = len(kblocks)
                for ki, b in enumerate(kblocks):
                    ps = psA.next()
                    kb.mm(ps[:, :n], [(kt[0:96, b * 128:(b + 1) * 128], q[0:96, :n])], (kt, q), (ps,))
                    p = pr.next()
                    kb.act(p[:, :n], ps[:, :n], AF.Exp, (ps,), (p,), scale=SC)
                    kb.mm(pso[0:65, :n], [(vt[:, b, :], p[:, :n])], (vt, p), (pso,), first=(ki == 0), last=(ki == nk - 1))
                o = orr.next()
                attn_finish(kb, g, pso, n, o[0:64, :n], o)
                kb.dma("pool", yT[3, h // 2, (h % 2) * 64:(h % 2) * 64 + 64, c0:c0 + n], o[0:64, :n], (o,),
                       [kb.dbuf("yT", 3, ti)], "do%d" % (orr.i % 2))


def phase_M(nc, kb, cfg, l, env):
    T, TT = cfg.T, cfg.TT
    g = env
    last = (l == cfg.depth - 1)
    psrot, modG = g["psrot"], g["modG"]
    sb, sbrot = g["sb"], g["sbrot"]
    xs, hT, yT = g["xs"], g["hT"], g["yT"]
    w_in, w_branch, w_out = g["w_in"], g["w_branch"], g["w_out"]
    with contextlib.ExitStack() as st:
        Wz = sb("mWz", [128, 8, 6144], BF16, st)
        Wb = sb("mWb", [128, 16, 1024], BF16, st)
        Wo = sb("mWo", [128, 8, 1024], BF16, st)
        for k in range(8):
            kb.dma("pool", Wz[:, k, :], w_in[l, k * 128:(k + 1) * 128, O_Z:NIN], (), (Wz,), "mW")
            kb.dma("pool", Wo[:, k, :], w_out[l, k * 128:(k + 1) * 128, :], (), (Wo,), "mW")
        for i in range(4):
            for k in range(4):
                kb.dma("pool", Wb[:, i * 4 + k, :], w_branch[l, i, k * 128:(k + 1) * 128, :], (), (Wb,), "mW")
        hr = sbrot("mh", [128, 8, 512], BF16, 1, st)
        yr = sbrot("my", [128, 16, 512], BF16, 1, st)
        szr = sbrot("msz", [128, 512], BF16, 2, st)
        sgr = sbrot("msg", [128, 512], F32, 2, st)
        tmr = sbrot("mtm", [128, 512], F32, 2, st)
        mar = sbrot("mma", [128, 512], F32, 2, st)
        mbr = sbrot("mmb", [128, 8, 512], BF16, 1, st)
        xr = sbrot("mx", [128, 512], F32, 2, st)
        tl = cfg.tiles[:-1] if last else cfg.tiles
        for ti, (c0, n) in enumerate(tl):
            isc = 1 if c0 >= T else 0
            h = hr.next(); y = yr.next(); mb = mbr.next()
            kb.dma("sp", h[:, :, :n], hT[:, :, c0:c0 + n].rearrange("k p n -> p k n"), [kb.dbuf("hT", ti)], (h,),
                   "mh0")
            for i in range(4):
                if cfg.branches[i]:
                    kb.dma("sp", y[:, i * 4:(i + 1) * 4, :n], yT[i, :, :, c0:c0 + n].rearrange("k p n -> p k n"),
                           [kb.dbuf("yT", i, ti)], (y,), "my0")
                else:
                    kb.memset("pool", y[:, i * 4:(i + 1) * 4, :n], 0.0, (y,))
            for j in range(16):
                ps = psrot.next()
                kb.mm(ps[:, :n], [(Wz[:, k, j * 128:(j + 1) * 128], h[:, k, :n]) for k in range(8)], (Wz, h), (ps,))
                sz = szr.next()
                kb.act(sz[:, :n], ps[:, :n], AF.Silu, (ps,), (sz,))
                kb.tt("dve", y[:, j, :n], y[:, j, :n], sz[:, :n], ALU.mult, (y, sz), (y,))
            for d in range(8):
                ma = mar.next()
                for i in range(4):
                    ps = psrot.next()
                    gc = 2048 + i * 1024 + d * 128
                    kb.mm(ps[:, :n], [(Wz[:, k, gc:gc + 128], h[:, k, :n]) for k in range(8)], (Wz, h), (ps,))
                    sg = sgr.next()
                    kb.act(sg[:, :n], ps[:, :n], AF.Sigmoid, (ps,), (sg,))
                    ps = psrot.next()
                    kb.mm(ps[:, :n], [(Wb[:, i * 4 + k, d * 128:(d + 1) * 128], y[:, i * 4 + k, :n]) for k in range(4)],
                          (Wb, y), (ps,))
                    if i == 0:
                        kb.tt("dve", ma[:, :n], ps[:, :n], sg[:, :n], ALU.mult, (ps, sg), (ma,))
                    else:
                        tm = tmr.next()
                        kb.tt("dve", tm[:, :n], ps[:, :n], sg[:, :n], ALU.mult, (ps, sg), (tm,))
                        kb.tt("pool", ma[:, :n], ma[:, :n], tm[:, :n], ALU.add, (ma, tm), (ma,))
                kb.copy("act", mb[:, d, :n], ma[:, :n], (ma,), (mb,))
            for d in range(8):
                ps = psrot.next()
                kb.mm(ps[:, :n], [(Wo[:, k, d * 128:(d + 1) * 128], mb[:, k, :n]) for k in range(8)], (Wo, mb), (ps,))
                x_t = xr.next()
                kb.dma("sp", x_t[:, :n], xs[d, :, c0:c0 + n], [kb.dbuf("xs", d)], (x_t,), "mx%d" % (d % 2))
                kb.stt("dve", x_t[:, :n], ps[:, :n], modG[:, l, d, isc:isc + 1], x_t[:, :n], ALU.mult, ALU.add,
                       (ps, x_t, modG), (x_t,))
                kb.dma("pool", xs[d, :, c0:c0 + n], x_t[:, :n], (x_t,), [kb.dbuf("xs", d)], "mxo%d" % (d % 2))


def rope_tables(T):
    TT = T + CTX
    t = np.arange(T)
    row = (t // 64).astype(np.float32)
    col = (t % 64).astype(np.float32)

    def tab(q4, base_rows):
        freqs = (10000.0 ** (-np.arange(q4, dtype=np.float32) / q4)).astype(np.float32)
        cos = np.ones((4 * q4, TT), np.float32)
        sin = np.zeros((4 * q4, TT), np.float32)
        for part, pos in ((0, row), (1, col)):
            ang = (pos[None, :] * freqs[:, None]).astype(np.float32)
            for r in range(2):
                r0 = part * 2 * q4 + r * q4
                cos[r0:r0 + q4, :T] = np.cos(ang)
                sin[r0:r0 + q4, :T] = np.sin(ang)
        return cos, sin
    cw, sw = tab(16, 0)
    cosW = np.concatenate([cw, cw], 0)
    sinW = np.concatenate([sw, sw], 0)
    cm, sm = tab(8, 0)
    cosM = np.concatenate([np.ones((64, TT), np.float32), cm], 0)
    sinM = np.concatenate([np.zeros((64, TT), np.float32), sm], 0)
    return cosW, sinW, cosM, sinM


def colpack(v, n):
    return np.ascontiguousarray(np.asarray(v, np.float32).reshape(n, 128).T)


def pack_vecs(inp, L):
    out = np.zeros((L, 128, NVEC), np.float32)
    for l in range(L):
        def put(name, arr):
            o, n = VEC[name]
            out[l, :, o:o + n] = arr
        put("norm_w", colpack(inp["norm_w"][l], 8))
        put("ada_b", colpack(inp["ada_b"][l], 24))
        put("mu", colpack(inp["rwkv_mu"][l], 13))
        put("w0", np.concatenate([colpack(inp["rwkv_w0"][l, d], 4) for d in range(2)], 1))
        put("a0", np.concatenate([colpack(inp["rwkv_a0"][l, d], 4) for d in range(2)], 1))
        put("k_k", colpack(inp["rwkv_k_k"][l], 4))
        put("k_a", colpack(inp["rwkv_k_a"][l], 4))
        put("r_k", colpack(np.asarray(inp["rwkv_r_k"][l]).reshape(-1), 4))
        put("conv_b", colpack(inp["conv_b"][l], 4))
        put("ln_w", colpack(inp["conv_ln_w"][l], 4))
        put("ln_b", colpack(inp["conv_ln_b"][l], 4))
        put("q_norm", colpack(inp["mla_q_norm"][l], 2))
        put("kv_norm", colpack(inp["mla_kv_norm"][l], 1))
        cw = np.asarray(inp["conv_w"][l], np.float32)
        put("conv_w", np.concatenate([colpack(cw[k], 4) for k in range(31)], 1))
    return out


def make_in_maps(inp, T, L, nb):
    cosW, sinW, cosM, sinM = rope_tables(T)
    vecs = pack_vecs(inp, L)
    kk = np.arange(128)[:, None]; qq = np.arange(128)[None, :]
    wmask = np.stack([np.tile((kk >= qq).astype(np.float32), (1, 4)), np.tile((kk <= qq).astype(np.float32), (1, 4))], 1)
    sk = np.asarray(inp["attn_sink"], np.float32)[:L]
    sinkrow = np.ascontiguousarray(np.repeat(sk.reshape(L, 2, 1, 4), 128, axis=-1))
    pp = np.arange(128)[:, None]; ff = np.arange(128)[None, :]
    same = (pp // 64) == (ff // 64)
    MU = (same & (pp < ff)).astype(np.float32); MUI = (same & (pp <= ff)).astype(np.float32)
    ML = (same & (pp > ff)).astype(np.float32); MLI = (same & (pp >= ff)).astype(np.float32)
    rmask = np.ascontiguousarray(np.stack([np.concatenate([MU, MUI, MU, MUI], 1), np.concatenate([ML, MLI, ML, MLI], 1),
                                           np.tile(ML, (1, 4)), np.tile(MU, (1, 4))], 1))
    bdones = same.astype(np.float32)
    resetm = np.ascontiguousarray(np.tile((np.arange(512) % 64 != 0).astype(np.float32)[None, :], (128, 1)))
    gnbc = np.ascontiguousarray(np.stack([np.stack([np.tile(np.asarray(inp[k][l_], np.float32)[None, :], (128, 1))
                                                    for k in ("rwkv_gn_w", "rwkv_gn_b")], 0) for l_ in range(L)], 0))
    maps = []
    f = lambda a: np.ascontiguousarray(np.asarray(a, np.float32))
    for b in range(nb):
        xfull = np.concatenate([np.asarray(inp["x"][b], np.float32), np.asarray(inp["ctx"][b], np.float32)], 0)
        xin = np.ascontiguousarray(xfull.T.reshape(8, 128, T + CTX))
        cvec = np.stack([colpack(inp["c"][b], 8), colpack(inp["c_ctx"], 8)], -1)
        maps.append({
            "xin": xin, "cvec": np.ascontiguousarray(cvec), "vecs": vecs,
            "ada_w": f(inp["ada_w"][:L]), "w_in": f(inp["w_in"][:L]), "w_branch": f(inp["w_branch"][:L]),
            "w_out": f(inp["w_out"][:L]), "q_up": f(inp["mla_q_up"][:L]), "kv_up": f(inp["mla_kv_up"][:L]),
            "fnw": colpack(inp["final_norm_w"], 8),
            "cosW": cosW, "sinW": sinW, "cosM": cosM, "sinM": sinM,
            "ident": np.eye(128, dtype=np.float32),
            "wmask": wmask, "sinkrow": sinkrow,
            "w_up": f(inp["rwkv_w_up"][:L]), "a_up": f(inp["rwkv_a_up"][:L]),
            "bdones": bdones, "rmask": rmask, "resetm": resetm, "gnbc": gnbc,
        })
    return maps


def kernel(**inp):
    T = inp["x"].shape[1]
    B = inp["x"].shape[0]
    L = inp["w_in"].shape[0]
    cfg = Cfg(T, L)
    nc = build(cfg)
    maps = make_in_maps(inp, T, L, B)
    in_maps = [maps[i] if i < B else maps[0] for i in range(8)]
    res = run_bass_kernel_spmd(nc, in_maps, core_ids=list(range(8)))
    out = np.stack([res.results[b]["yout"].reshape(D, T).T for b in range(B)], 0)
    return np.ascontiguousarray(out.astype(np.float32))
```
